# Optimizing a Trainium2 kernel written in Bass

```python
import math
import jax, jax.numpy as jnp
from jax import lax
import numpy as np

D_MODEL = 2048
BATCH = 8
SEQ = 2048
DEPTH = 1
DEC_BATCH = 4
DEC_SEQ = 2048
PAST_LEN = 128

HEAD_DIM = 128
N_HEADS = 8
N_KV_HEADS = 2
ATTN_WIDTH = N_HEADS * HEAD_DIM
KV_WIDTH = N_KV_HEADS * HEAD_DIM
N_SGU_GROUPS = 8
SGU_GROUP_DIM = 128
SGU_WIDTH = N_SGU_GROUPS * SGU_GROUP_DIM
MIX_WIDTH = ATTN_WIDTH + SGU_WIDTH
IN_WIDTH = ATTN_WIDTH + 2 * KV_WIDTH + 2 * SGU_WIDTH
BLOCK = 128
WINDOW = 128
D_FF = int(math.ceil(8 * D_MODEL / 3 / 256) * 256)
LN_EPS = 1e-5
NEG_INF = -1e30
DEEPNORM_ALPHA = (2.0 * DEPTH) ** 0.25
DEEPNORM_BETA = (8.0 * DEPTH) ** -0.25

kernel_name = "hybrid_sgu_window_gqa_deepnorm_encoder"


def _layer_norm(x, g, b):
    xf = x.astype(jnp.float32)
    mu = jnp.mean(xf, axis=-1, keepdims=True)
    xc = xf - mu
    var = jnp.mean(xc * xc, axis=-1, keepdims=True)
    y = xc * lax.rsqrt(var + LN_EPS) * g.astype(jnp.float32) + b.astype(jnp.float32)
    return y.astype(x.dtype)


def _alibi_slopes():
    h = np.arange(1, N_HEADS + 1, dtype=np.float32)
    return jnp.asarray(2.0 ** (-8.0 * h / N_HEADS), dtype=jnp.float32)


def _window_attention(q, k, v, sink):
    B, S, H, D = q.shape
    Hkv = k.shape[2]
    G = H // Hkv
    NC = S // BLOCK
    qb = q.reshape(B, NC, BLOCK, Hkv, G, D)
    pad = ((0, 0), (BLOCK, BLOCK), (0, 0), (0, 0))
    kp = jnp.pad(k, pad).reshape(B, NC + 2, BLOCK, Hkv, D)
    vp = jnp.pad(v, pad).reshape(B, NC + 2, BLOCK, Hkv, D)
    kb = jnp.concatenate([kp[:, :-2], kp[:, 1:-1], kp[:, 2:]], axis=2)
    vb = jnp.concatenate([vp[:, :-2], vp[:, 1:-1], vp[:, 2:]], axis=2)
    scale = 1.0 / math.sqrt(D)
    s = jnp.einsum('bcqhgd,bckhd->bchgqk', qb, kb,
                   preferred_element_type=jnp.float32) * scale
    rel = jnp.arange(3 * BLOCK)[None, :] - BLOCK - jnp.arange(BLOCK)[:, None]
    dist = jnp.abs(rel)
    kpos = jnp.arange(NC)[:, None] * BLOCK - BLOCK + jnp.arange(3 * BLOCK)[None, :]
    valid = (dist <= WINDOW)[None] & ((kpos >= 0) & (kpos < S))[:, None, :]
    slopes = _alibi_slopes().reshape(Hkv, G)
    s = s - slopes[:, :, None, None] * dist.astype(jnp.float32)
    s = jnp.where(valid[None, :, None, None], s, NEG_INF)
    sk = sink.astype(jnp.float32).reshape(Hkv, G)[None, None, :, :, None]
    m = jnp.maximum(jnp.max(s, axis=-1), sk)
    p = jnp.exp(s - m[..., None])
    den = jnp.sum(p, axis=-1) + jnp.exp(sk - m)
    w = (p / den[..., None]).astype(v.dtype)
    o = jnp.einsum('bchgqk,bckhd->bcqhgd', w, vb)
    return o.reshape(B, S, H * D)


def _chunk_sgu(z, ln_g, ln_b, w_s, b_s):
    B, S, _ = z.shape
    NC = S // BLOCK
    u, gv = z[..., :SGU_WIDTH], z[..., SGU_WIDTH:]
    gv = _layer_norm(gv, ln_g, ln_b)
    gv = gv.reshape(B, NC, BLOCK, N_SGU_GROUPS, SGU_GROUP_DIM)
    mixed = jnp.einsum('gts,bcsgd->bctgd', w_s.astype(gv.dtype), gv) + b_s.T[:, :, None].astype(gv.dtype)
    return u * mixed.reshape(B, S, SGU_WIDTH)


def _layer(x, w_in, ln_sgu_g, ln_sgu_b, w_s, b_s, attn_sink, w_o,
           ln1_g, ln1_b, w_gate, w_up, w_down, ln2_g, ln2_b):
    B, S, _ = x.shape
    h = x @ w_in
    o0 = ATTN_WIDTH
    o1 = o0 + KV_WIDTH
    o2 = o1 + KV_WIDTH
    q = h[..., :o0].reshape(B, S, N_HEADS, HEAD_DIM)
    k = h[..., o0:o1].reshape(B, S, N_KV_HEADS, HEAD_DIM)
    v = h[..., o1:o2].reshape(B, S, N_KV_HEADS, HEAD_DIM)
    attn = _window_attention(q, k, v, attn_sink)
    sgu = _chunk_sgu(jax.nn.gelu(h[..., o2:]), ln_sgu_g, ln_sgu_b, w_s, b_s)
    mix = jnp.concatenate([attn, sgu], axis=-1) @ w_o
    x = _layer_norm(DEEPNORM_ALPHA * x + mix, ln1_g, ln1_b)
    ff = (jax.nn.silu(x @ w_gate) * (x @ w_up)) @ w_down
    x = _layer_norm(DEEPNORM_ALPHA * x + ff, ln2_g, ln2_b)
    return x


def setup_inputs(seed: int = 0) -> dict:
    key = jax.random.key(seed)
    ks = jax.random.split(key, 16)
    f32 = jnp.float32
    nrm = lambda k, shape, s: jax.random.normal(k, shape, f32) * s
    return {
        "x_prompt": jax.random.normal(ks[0], (BATCH, SEQ, D_MODEL), f32),
        "x_sample": jax.random.normal(ks[1], (DEC_BATCH, DEC_SEQ, D_MODEL), f32),
        "w_in": nrm(ks[2], (DEPTH, D_MODEL, IN_WIDTH), D_MODEL ** -0.5),
        "ln_sgu_g": 1.0 + nrm(ks[3], (DEPTH, SGU_WIDTH), 0.01),
        "ln_sgu_b": nrm(ks[4], (DEPTH, SGU_WIDTH), 0.01),
        "w_s": nrm(ks[5], (DEPTH, N_SGU_GROUPS, BLOCK, BLOCK), BLOCK ** -0.5),
        "b_s": 1.0 + nrm(ks[6], (DEPTH, N_SGU_GROUPS, BLOCK), 0.01),
        "attn_sink": nrm(ks[7], (DEPTH, N_HEADS), 0.5),
        "w_o": nrm(ks[8], (DEPTH, MIX_WIDTH, D_MODEL), MIX_WIDTH ** -0.5 * DEEPNORM_BETA),
        "ln1_g": 1.0 + nrm(ks[9], (DEPTH, D_MODEL), 0.01),
        "ln1_b": nrm(ks[10], (DEPTH, D_MODEL), 0.01),
        "w_gate": nrm(ks[11], (DEPTH, D_MODEL, D_FF), D_MODEL ** -0.5),
        "w_up": nrm(ks[12], (DEPTH, D_MODEL, D_FF), D_MODEL ** -0.5),
        "w_down": nrm(ks[13], (DEPTH, D_FF, D_MODEL), D_FF ** -0.5 * DEEPNORM_BETA),
        "ln2_g": 1.0 + nrm(ks[14], (DEPTH, D_MODEL), 0.01),
        "ln2_b": nrm(ks[15], (DEPTH, D_MODEL), 0.01),
    }


def reference(x_prompt, x_sample, w_in, ln_sgu_g, ln_sgu_b, w_s, b_s, attn_sink, w_o,
              ln1_g, ln1_b, w_gate, w_up, w_down, ln2_g, ln2_b):
    y_prompt = x_prompt
    y_sample = x_sample
    for l in range(DEPTH):
        p = (w_in[l], ln_sgu_g[l], ln_sgu_b[l], w_s[l], b_s[l], attn_sink[l], w_o[l],
             ln1_g[l], ln1_b[l], w_gate[l], w_up[l], w_down[l], ln2_g[l], ln2_b[l])
        y_prompt = _layer(y_prompt, *p)
        y_sample = _layer(y_sample, *p)
    return (y_prompt, y_sample)
```

```python
import math
import numpy as np
import concourse.bass as bass
import concourse.mybir as mybir
from concourse.bass_utils import run_bass_kernel_spmd

F32 = mybir.dt.float32
BF16 = mybir.dt.bfloat16
AF = mybir.ActivationFunctionType
ALU = mybir.AluOpType

D = 2048
KC = 16
NBLK = 4
TT = 512
NTILE = 6
NTOK = NTILE * TT
DFF = 5632
NG = 11
RS = 10
PIECE = 2048
NPIECE = 28 + 32 + 12 * NG
CONV_BATCH = 8
CAST_TILES = 2
NEGB = -1.0e4
ALPHA = float((2.0 * 1) ** 0.25)
QSCALE = 1.0 / math.sqrt(128.0)
LN_EPS = 1e-5


class _Rec:
    __slots__ = ("ivs", "lo", "hi", "w", "rs")

    def __init__(self, ivs):
        self.ivs = ivs
        self.lo = min(a for a, _ in ivs)
        self.hi = max(b for _, b in ivs)
        self.w = None
        self.rs = []


class _Op:
    __slots__ = ("eng", "fn", "reads", "writes", "idx", "deps", "signal", "semval", "dma_key",
                 "dma_val", "batch")

    def __init__(self, eng, fn, reads, writes, dma_key):
        self.eng = eng
        self.fn = fn
        self.reads = reads
        self.writes = writes
        self.deps = {}
        self.signal = False
        self.semval = None
        self.dma_key = dma_key
        self.dma_val = None
        self.batch = None


def _intervals(ap):
    dsz = mybir.dt.size(ap.dtype)
    space = str(ap.space)
    dims = list(ap.ap)
    off = ap.offset
    if space != "DRAM":
        pstride = dims[0][0]
        if pstride > 0:
            off = off % pstride
        dims = dims[1:]
    region = (space, ap.tensor.name)
    dims = [(abs(s), c) for s, c in dims if c > 1]
    if not dims:
        ivs = [(off * dsz, (off + 1) * dsz)]
    else:
        dims.sort(key=lambda sc: sc[0])
        s0, c0 = dims[0]
        run = (c0 - 1) * s0 + 1
        outer = dims[1:]
        n_outer = 1
        for _, c in outer:
            n_outer *= c
        if n_outer <= 32:
            starts = [off]
            for s, c in outer:
                starts = [st + i * s for st in starts for i in range(c)]
            ivs = [(st * dsz, (st + run) * dsz) for st in starts]
        else:
            ext = run + sum((c - 1) * s for s, c in outer)
            ivs = [(off * dsz, (off + ext) * dsz)]
    if space == "PSUM":
        lo = min(a for a, _ in ivs) // 2048 * 2048
        hi = (max(b for _, b in ivs) + 2047) // 2048 * 2048
        ivs = [(lo, hi)]
    return region, tuple(ivs)


def _overlap(r, ivs, lo, hi):
    if r.hi <= lo or hi <= r.lo:
        return False
    for a, b in r.ivs:
        for c, d in ivs:
            if a < d and c < b:
                return True
    return False


class Prog:
    ENGS = ("pe", "act", "dve", "pool", "sp")

    def __init__(self):
        self.ops = []
        self.regions = {}
        self.cur_batch = None
        self.final_keys = []

    def op(self, eng, fn, reads=(), writes=(), dma_key=None):
        o = _Op(eng, fn, list(reads), list(writes), dma_key)
        o.idx = len(self.ops)
        if dma_key is not None and self.cur_batch is not None:
            o.batch = self.cur_batch
            self.cur_batch.append(o)
        self.ops.append(o)
        psum_reads = []
        for ap in o.reads:
            region, ivs = _intervals(ap)
            if region[0] == "PSUM":
                psum_reads.append((region, ivs))
                continue
            self._access(o, region, ivs, False)
        for ap in o.writes:
            region, ivs = _intervals(ap)
            self._access(o, region, ivs, True)
        for region, ivs in psum_reads:
            self._access(o, region, ivs, True, kind_raw=True)
        return o

    def _access(self, o, region, ivs, is_write, kind_raw=False):
        recs = self.regions.setdefault(region, {})
        lo = min(a for a, _ in ivs)
        hi = max(b for _, b in ivs)
        mine = recs.get(ivs)
        if mine is None:
            mine = _Rec(ivs)
            recs[ivs] = mine
        for r in recs.values():
            if r is not mine and not _overlap(r, ivs, lo, hi):
                continue
            if r.w is not None and r.w is not o:
                k = "raw" if (not is_write or kind_raw) else "waw"
                self._dep(o, r.w, k)
            if is_write:
                for rd in r.rs:
                    if rd is not o:
                        self._dep(o, rd, "war")
        if is_write:
            for r in list(recs.values()):
                if r is mine or _overlap(r, ivs, lo, hi):
                    r.rs = []
                    r.w = o
        else:
            mine.rs.append(o)

    @staticmethod
    def _dep(o, d, kind):
        prev = o.deps.get(d)
        if prev is None or kind == "raw":
            o.deps[d] = kind

    def batch_begin(self):
        self.cur_batch = []

    def batch_end(self):
        self.cur_batch = None

    def emit(self, nc, sems):
        ops = self.ops
        for o in ops:
            keep = {}
            for d, kind in o.deps.items():
                if d.dma_key is None and o.dma_key is None and d.eng == o.eng:
                    if o.eng == "pe":
                        continue
                    if kind != "raw":
                        continue
                keep[d] = kind
            o.deps = keep
            for d in keep:
                d.signal = True
        cnt = {e: 0 for e in self.ENGS}
        dcnt = {}
        for o in ops:
            if o.dma_key is not None:
                dcnt[o.dma_key] = dcnt.get(o.dma_key, 0) + 16
                o.dma_val = dcnt[o.dma_key]
            elif o.signal:
                cnt[o.eng] += 1
                o.semval = cnt[o.eng]
        for o in ops:
            if o.batch is not None:
                o.dma_val = max(m.dma_val for m in o.batch)
        self.final_dma = dict(dcnt)
        self.counts = cnt
        streams = {e: [o for o in ops if o.eng == e] for e in self.ENGS}

        def run(eng_name, e):
            seen = {}
            for o in streams[eng_name]:
                waits = {}
                for d in o.deps:
                    if d.dma_key is not None:
                        key, val = ("dma", d.dma_key), d.dma_val
                    else:
                        key, val = ("eng", d.eng), d.semval
                    if waits.get(key, 0) < val:
                        waits[key] = val
                for key, val in waits.items():
                    if seen.get(key, 0) < val:
                        e.wait_ge(sems[key], val)
                        seen[key] = val
                ins = o.fn(e)
                if o.dma_key is not None:
                    ins.then_inc(sems[("dma", o.dma_key)], 16)
                elif o.signal:
                    ins.then_inc(sems[("eng", eng_name)], 1)
            if eng_name == "sp":
                for key in self.final_keys:
                    e.wait_ge(sems[("dma", key)], self.final_dma[key])
        return run

    def dma_keys(self):
        return sorted({o.dma_key for o in self.ops if o.dma_key is not None}, key=str)


def build(ntiles=NTILE, stop_after="F", dbg=None):
    dbg = dbg or []
    nc = bass.Bass("TRN2", target_bir_lowering=False)
    P = Prog()

    def din(name, shape, dt=F32):
        return nc.dram_tensor(name, list(shape), dt, kind="ExternalInput").ap()

    xin = din("xin", [NTILE, 6, 128, D])
    hb_d = din("hb", [128, 12])
    w_in = din("w_in", [D, 3584])
    w_o = din("w_o", [D, D])
    w_g = din("w_gate", [D, DFF])
    w_u = din("w_up", [D, DFF])
    w_d = din("w_down", [DFF, D])
    distm_d = din("distm", [128, 3 * 128])
    nslope_d = din("nslope", [128, 8])
    sink_d = din("sink", [128, 8])
    bsb_d = din("bsb", [128, 1024])
    lnsg_d = din("lnsg", [128, 1024])
    lnsb_d = din("lnsb", [128, 1024])
    ln1_d = din("ln1", [128, 2 * D])
    ln2_d = din("ln2", [128, 2 * D])
    wst_d = din("wst", [128, 1024])
    ident_d = din("ident", [128, 128])
    yc = nc.dram_tensor("yc", [NTOK, D], F32, kind="ExternalOutput").ap()
    wsc = nc.dram_tensor("wsc", [NPIECE, 128, PIECE], BF16, kind="Internal").ap()
    taps = {}

    import contextlib
    es = contextlib.ExitStack()
    with es:
        def sb(name, shape, dt):
            return es.enter_context(nc.sbuf_tensor("sb_" + name, list(shape), dt))

        ring = sb("ring", [128, RS, PIECE], BF16)
        xs = sb("xs", [128, 2, 1024], F32)
        xTm = sb("xTm", [128, KC, TT], BF16)
        xTh = sb("xTh", [128, KC, 256], BF16)
        QU = sb("QU", [128, KC, TT], BF16)
        kT = sb("kT", [128, 2, 768], BF16)
        vv = sb("vv", [128, 6, 256], BF16)
        gv = sb("gv", [128, NBLK, 1024], BF16)
        tmp4 = sb("tmp4", [128, 2, 1024], F32)
        ein = sb("ein", [128, 3, 512], F32)
        pT = sb("pT", [128, 6, 512], BF16)
        rden = sb("rden", [128, 2, 512], F32)
        x1 = sb("x1", [128, NBLK, D], F32)
        sg = sb("sg", [128, 2, 512], F32)
        h1T = sb("h1T", [128, 2, 4, TT], BF16)
        lnp = sb("lnp", [128, 2, D], F32)
        distm = sb("distm", [128, 3, 128], F32)
        nslope = sb("nslope", [128, 8], F32)
        esk = sb("esk", [128, 8], F32)
        hb = sb("hbias", [128, 12], F32)
        bsb = sb("bsb", [128, 1024], F32)
        lnsg = sb("lnsg", [128, 1024], F32)
        lnsb = sb("lnsb", [128, 1024], F32)
        wsT = sb("wsT", [128, 8, 128], BF16)
        ident = sb("ident", [128, 128], F32)
        ones = sb("ones", [128, 128], BF16)
        stat = sb("stat", [128, 8, 32], F32)
        lnr = sb("lnr", [128, 4, 8], F32)
        ps = es.enter_context(nc.psum_tensor("ps", [128, 8, 512], F32))

        st = {"bank": 0, "stat": 0, "xs": 0, "tmp4": 0, "ein": 0, "pT": 0, "rden": 0, "sg": 0,
              "nload": 0, "cid": 0, "done": -1, "lnr": 0, "sbank": 0}

        def bank():
            b = st["bank"]
            st["bank"] = (b + 1) % 8
            return b

        def bank2():
            if st["bank"] % 2:
                st["bank"] = (st["bank"] + 1) % 8
            b = st["bank"]
            st["bank"] = (b + 2) % 8
            return b

        def bank4():
            if st["bank"] % 4:
                st["bank"] = (st["bank"] + 4 - st["bank"] % 4) % 8
            b = st["bank"]
            st["bank"] = (b + 4) % 8
            return b

        def rot(name, n):
            v = st[name]
            st[name] = (v + 1) % n
            return v

        def cid():
            st["cid"] += 1
            return st["cid"]

        win_r = w_in.rearrange("(k p) n -> p k n", p=128)
        wo_r = w_o.rearrange("(k p) n -> p k n", p=128)
        wg_r = w_g.rearrange("(k p) n -> p k n", p=128)
        wu_r = w_u.rearrange("(k p) n -> p k n", p=128)
        pieces = []

        def colpiece(wr, c0):
            pieces.append((wr[:, :, c0:c0 + 128], 16, 128))

        for m in range(2):
            colpiece(win_r, 1024 + m * 128)
        for j in range(2):
            pieces.append((win_r[:, j * 8:(j + 1) * 8, 1280:1536], 8, 256))
        for m in range(8):
            colpiece(win_r, m * 128)
        for n in range(2):
            for kq in range(4):
                pieces.append((win_r[:, 4 * kq:4 * kq + 4, 2560 + n * 512:2560 + (n + 1) * 512], 4, 512))
        for m in range(8):
            colpiece(win_r, 1536 + m * 128)
        for bp in range(2):
            for nh in range(2):
                for kp in range(8):
                    pieces.append((wo_r[:, 2 * kp:2 * kp + 2, nh * 1024:(nh + 1) * 1024], 2, 1024))
        ffn_order = []
        for g in range(NG):
            ffn_order.append(("gu", g))
            if g >= 1:
                ffn_order.append(("d", g - 1))
        ffn_order.append(("d", NG - 1))
        ffn_base = {}
        for kind, g in ffn_order:
            ffn_base[(kind, g)] = len(pieces)
            if kind == "gu":
                for wr in (wg_r, wu_r):
                    for kq in range(4):
                        pieces.append((wr[:, 4 * kq:4 * kq + 4, g * 512:(g + 1) * 512], 4, 512))
            else:
                for kk in range(4):
                    c = g * 4 + kk
                    pieces.append((w_d[c * 128:(c + 1) * 128, :], 1, 2048))
        assert len(pieces) == NPIECE

        def emit_conversion():
            for i0 in range(0, NPIECE, CONV_BATCH):
                P.batch_begin()
                for i in range(i0, min(NPIECE, i0 + CONV_BATCH)):
                    src, a, b = pieces[i]
                    if a == 1:
                        dst = wsc[i]
                    else:
                        dst = wsc[i].rearrange("p (a b) -> p a b", b=b)
                    P.op("pool", lambda e, dst=dst, src=src: e.dma_start(out=dst, in_=src),
                         reads=[src], writes=[dst], dma_key=("conv", i0 // CONV_BATCH))
                P.batch_end()

        total_pieces = ntiles * NPIECE

        def lookahead():
            lim = min(total_pieces, st["done"] + 1 + RS)
            while st["nload"] < lim:
                i = st["nload"]
                slot = i % RS
                dst = ring[:, slot, :]
                if i < NPIECE * CAST_TILES:
                    src, a, b = pieces[i % NPIECE]
                    dstv = dst if a == 1 else dst.rearrange("p (a b) -> p a b", b=b)
                    P.op("pool", lambda e, dstv=dstv, src=src: e.dma_start(out=dstv, in_=src),
                         reads=[src], writes=[dst], dma_key=("ring", slot))
                    if ntiles > CAST_TILES and (i % NPIECE) % CAST_TILES == i // NPIECE:
                        wdst = wsc[i % NPIECE]
                        P.op("sp", lambda e, wdst=wdst, dst=dst: e.dma_start(out=wdst, in_=dst),
                             reads=[dst], writes=[wdst], dma_key=("wst", slot))
                else:
                    src = wsc[i % NPIECE]
                    P.op("sp", lambda e, dst=dst, src=src: e.dma_start(out=dst, in_=src),
                         reads=[src], writes=[dst], dma_key=("ring", slot))
                st["nload"] += 1

        def piece_ap(gidx):
            assert gidx < st["done"] + 1 + RS
            lookahead()
            return ring[:, gidx % RS, :]

        def done(gidx):
            st["done"] = max(st["done"], gidx)
            lookahead()

        def cload(dst, src):
            P.op("sp", lambda e: e.dma_start(out=dst, in_=src), reads=[src], writes=[dst],
                 dma_key=("c", cid()))

        def emit_consts():
            cload(ident[:], ident_d)
            cload(hb[:], hb_d)
            cload(distm[:].rearrange("p a b -> p (a b)"), distm_d)
            cload(nslope[:], nslope_d)
            cload(esk[:], sink_d)
            cload(bsb[:], bsb_d)
            cload(lnsg[:], lnsg_d)
            cload(lnsb[:], lnsb_d)
            cload(tmp4[:, 0, :], wst_d)
            P.op("dve", lambda e: e.memset(ones[:], 1.0), writes=[ones[:]])
            P.op("dve", lambda e: e.tensor_copy(out=wsT[:].rearrange("p a b -> p (a b)"), in_=tmp4[:, 0, :]),
                 reads=[tmp4[:, 0, :]], writes=[wsT[:]])
            P.op("act", lambda e: e.activation(out=esk[:], in_=esk[:], func=AF.Exp),
                 reads=[esk[:]], writes=[esk[:]])

        def xT_dst(blk, k0, k1):
            if blk == 0:
                return xTh[:, k0:k1, 0:128]
            if blk == 5:
                return xTh[:, k0:k1, 128:256]
            return xTm[:, k0:k1, (blk - 1) * 128:blk * 128]

        def a0_steps(t):
            items = [(blk, hf) for blk in range(6) for hf in range(2)]
            if t == 0:
                items = [it for it in items if it[0] in (0, 5)] + [it for it in items if it[0] not in (0, 5)]
                for b in range(NBLK):
                    src0 = xin[0, 1 + b]
                    P.op("sp", lambda e, b=b, src0=src0: e.dma_start(out=x1[:, b, :], in_=src0),
                         reads=[src0], writes=[x1[:, b, :]], dma_key=("xres", b))
            slots = []

            def load(i):
                blk, hf = items[i]
                if t == 0 and blk not in (0, 5):
                    slots.append(None)
                    return
                s = rot("xs", 2)
                slots.append(s)
                src = xin[t, blk, :, hf * 1024:(hf + 1) * 1024]
                dst = xs[:, s, :]
                P.op("sp", lambda e: e.dma_start(out=dst, in_=src), reads=[src], writes=[dst],
                     dma_key=("xs", s))

            load(0)
            load(1)
            for i, (blk, hf) in enumerate(items):
                s = slots[i]
                for q in range(2):
                    if t == 0:
                        b = bank()
                    else:
                        b = (st["bank"] + (4 - st["bank"] % 4) % 4) % 8 + q
                    k0 = hf * 8 + q * 4

                    if s is None:
                        srcx = x1[:, blk - 1, hf * 1024 + q * 512:hf * 1024 + (q + 1) * 512]
                    else:
                        srcx = xs[:, s, q * 512:(q + 1) * 512]

                    def tfn(e, srcx=srcx, b=b):
                        for j in range(4):
                            ins = e.transpose(out=ps[:, b, j * 128:(j + 1) * 128],
                                              in_=srcx[:, j * 128:(j + 1) * 128], identity=ident[:])
                        return ins
                    P.op("pe", tfn, reads=[srcx, ident[:]], writes=[ps[:, b, :]])
                    dst = xT_dst(blk, k0, k0 + 4)
                    src = ps[:, b, :].rearrange("p (a b) -> p a b", b=128)
                    P.op("act", lambda e, dst=dst, src=src: e.activation(out=dst, in_=src, func=AF.Copy),
                         reads=[src], writes=[dst])
                if i + 2 < len(items):
                    load(i + 2)
                yield

        def xT_blk(k, blk):
            if blk == 0:
                return xTh[:, k, 0:128]
            if blk == 5:
                return xTh[:, k, 128:256]
            return xTm[:, k, (blk - 1) * 128:blk * 128]

        def ln_group(bufs, n):
            gs = rot("lnr", 4)
            nchunk = n // 512
            means = []
            for i, buf in enumerate(bufs):
                s = rot("stat", 8)
                bst = stat[:, s, 0:6 * nchunk]
                mv = stat[:, s, 24:26]

                def f1(e, s=s, buf=buf):
                    for c in range(nchunk):
                        ins = e.bn_stats(out=stat[:, s, 6 * c:6 * c + 6], in_=buf[:, c * 512:(c + 1) * 512])
                    return ins
                P.op("dve", f1, reads=[buf], writes=[bst])
                P.op("dve", lambda e, mv=mv, bst=bst: e.bn_aggr(out=mv, in_=bst), reads=[bst], writes=[mv])
                P.op("dve", lambda e, s=s, i=i: e.tensor_scalar(out=lnr[:, gs, i:i + 1], in0=stat[:, s, 25:26],
                                                               scalar1=LN_EPS, scalar2=None, op0=ALU.add),
                     reads=[stat[:, s, 25:26]], writes=[lnr[:, gs, i:i + 1]])
                means.append(stat[:, s, 24:25])
            nb = len(bufs)
            P.op("act", lambda e: e.activation(out=lnr[:, gs, 4:4 + nb], in_=lnr[:, gs, 0:nb], func=AF.Sqrt),
                 reads=[lnr[:, gs, 0:nb]], writes=[lnr[:, gs, 4:4 + nb]])
            P.op("dve", lambda e: e.reciprocal(out=lnr[:, gs, 4:4 + nb], in_=lnr[:, gs, 4:4 + nb]),
                 reads=[lnr[:, gs, 4:4 + nb]], writes=[lnr[:, gs, 4:4 + nb]])
            return [(means[i], lnr[:, gs, 4 + i:5 + i]) for i in range(nb)]

        pend = {"on": False, "ops": []}

        def bank_guard(banks):
            for fn, reads, writes in pend["ops"]:
                for r in reads:
                    if str(r.space) == "PSUM" and (r.offset % 4096) // 512 in banks:
                        act_flush()
                        return

        def act_op(fn, reads, writes):
            if pend["on"]:
                pend["ops"].append((fn, reads, writes))
            else:
                P.op("act", fn, reads=reads, writes=writes)

        def act_flush():
            for fn, reads, writes in pend["ops"]:
                P.op("act", fn, reads=reads, writes=writes)
            pend["ops"] = []

        def q_piece(t, base, m, b):
            pc = piece_ap(base + 4 + m)
            bank_guard([b])

            def f(e):
                for k in range(KC):
                    ins = e.matmul(ps[:, b, :], lhsT=pc[:, k * 128:(k + 1) * 128], rhs=xTm[:, k, :],
                                   start=(k == 0), stop=(k == KC - 1))
                return ins
            P.op("pe", f, reads=[pc, xTm[:]], writes=[ps[:, b, :]])
            done(base + 4 + m)
            act_op(lambda e: e.activation(out=QU[:, m, :], in_=ps[:, b, :], func=AF.Copy, scale=QSCALE),
                   [ps[:, b, :]], [QU[:, m, :]])

        def phase_A(t, base):
            for hk in range(2):
                pc = piece_ap(base + hk)
                b1 = bank()
                b2 = bank()

                def f(e, pc=pc, b1=b1, b2=b2):
                    for k in range(KC):
                        e.matmul(ps[:, b1, :], lhsT=pc[:, k * 128:(k + 1) * 128], rhs=xTm[:, k, :],
                                 start=(k == 0), stop=(k == KC - 1))
                        ins = e.matmul(ps[:, b2, 0:256], lhsT=pc[:, k * 128:(k + 1) * 128], rhs=xTh[:, k, :],
                                       start=(k == 0), stop=(k == KC - 1))
                    return ins
                P.op("pe", f, reads=[pc, xTm[:], xTh[:]], writes=[ps[:, b1, :], ps[:, b2, :]])
                done(base + hk)
                P.op("act", lambda e, b1=b1, hk=hk: e.activation(out=kT[:, hk, 128:640], in_=ps[:, b1, :],
                                                                func=AF.Copy),
                     reads=[ps[:, b1, :]], writes=[kT[:, hk, 128:640]])
                dst = kT[:, hk, :].rearrange("p (a b) -> p a b", b=128)[:, 0:6:5, :]
                src = ps[:, b2, 0:256].rearrange("p (a b) -> p a b", b=128)
                P.op("act", lambda e, dst=dst, src=src: e.activation(out=dst, in_=src, func=AF.Copy),
                     reads=[src], writes=[dst])
            vb = [bank() for _ in range(6)]
            for j in range(2):
                pc = piece_ap(base + 2 + j).rearrange("p (a b) -> p a b", b=256)

                def f(e, pc=pc, j=j):
                    for kq in range(8):
                        k = j * 8 + kq
                        for blk in range(6):
                            ins = e.matmul(ps[:, vb[blk], 0:256], lhsT=xT_blk(k, blk), rhs=pc[:, kq, :],
                                           start=(k == 0), stop=(k == KC - 1))
                    return ins
                P.op("pe", f, reads=[pc, xTm[:], xTh[:]], writes=[ps[:, vb[blk], :] for blk in range(6)])
                done(base + 2 + j)
            for blk in range(6):
                P.op("act", lambda e, blk=blk: e.activation(out=vv[:, blk, :], in_=ps[:, vb[blk], 0:256],
                                                           func=AF.Copy),
                     reads=[ps[:, vb[blk], 0:256]], writes=[vv[:, blk, :]])
            for m in range(4):
                q_piece(t, base, m, bank())

        zg_alias = h1T[:].rearrange("p a b c -> p (a b c)").bitcast(F32).rearrange("p (a b) -> p a b", b=1024)

        def zg(b):
            return tmp4[:, b, :] if b < 2 else zg_alias[:, b - 2, :]

        mixA = xs[:].rearrange("p a b -> p (a b)").bitcast(BF16).rearrange("p (a b) -> p a b", b=512)

        def mixT_kb(k, b):
            if k < 8:
                return mixA[:, k, b * 128:(b + 1) * 128]
            return xTm[:, k, b * 128:(b + 1) * 128]

        def inproj_piece(t, base, idx):
            if idx < 8:
                n, kq = idx // 4, idx % 4
                pc = piece_ap(base + 12 + idx).rearrange("p (a b) -> p a b", b=512)
                if kq == 0:
                    bank_guard([0, 1, 2, 3])

                def f(e, pc=pc, kq=kq):
                    for kk in range(4):
                        k = 4 * kq + kk
                        for b in range(NBLK):
                            ins = e.matmul(ps[:, b, :], lhsT=xTm[:, k, b * 128:(b + 1) * 128], rhs=pc[:, kk, :],
                                           start=(k == 0), stop=(k == KC - 1))
                    return ins
                P.op("pe", f, reads=[pc, xTm[:, 4 * kq:4 * kq + 4, :]], writes=[ps[:, i, :] for i in range(4)])
                done(base + 12 + idx)
                if kq == 3:
                    for b in range(NBLK):
                        act_op(lambda e, b=b, n=n: e.activation(out=zg(b)[:, n * 512:(n + 1) * 512],
                                                               in_=ps[:, b, :], func=AF.Gelu_apprx_tanh),
                               [ps[:, b, :]], [zg(b)[:, n * 512:(n + 1) * 512]])
            else:
                m = idx - 8
                pc = piece_ap(base + 12 + idx)
                b = m % 4
                bank_guard([b])

                def f(e, pc=pc, b=b):
                    for k in range(KC):
                        ins = e.matmul(ps[:, b, :], lhsT=pc[:, k * 128:(k + 1) * 128], rhs=xTm[:, k, :],
                                       start=(k == 0), stop=(k == KC - 1))
                    return ins
                P.op("pe", f, reads=[pc, xTm[:]], writes=[ps[:, b, :]])
                done(base + 12 + idx)
                act_op(lambda e, b=b, m=m: e.activation(out=QU[:, 8 + m, :], in_=ps[:, b, :],
                                                       func=AF.Gelu_apprx_tanh),
                       [ps[:, b, :]], [QU[:, 8 + m, :]])

        gvst = {}

        def gv_ln_stage(b, stage):
            z = zg(b)
            if stage == 0:
                gs = rot("lnr", 4)
                s_ = rot("stat", 8)
                gvst[b] = (gs, s_)
                bst = stat[:, s_, 0:12]
                mv = stat[:, s_, 24:26]

                def f1(e):
                    e.bn_stats(out=stat[:, s_, 0:6], in_=z[:, 0:512])
                    return e.bn_stats(out=stat[:, s_, 6:12], in_=z[:, 512:1024])
                P.op("dve", f1, reads=[z], writes=[bst])
                P.op("dve", lambda e: e.bn_aggr(out=mv, in_=bst), reads=[bst], writes=[mv])
                P.op("dve", lambda e: e.tensor_scalar(out=lnr[:, gs, 0:1], in0=stat[:, s_, 25:26], scalar1=LN_EPS,
                                                      scalar2=None, op0=ALU.add),
                     reads=[stat[:, s_, 25:26]], writes=[lnr[:, gs, 0:1]])
                act_op(lambda e: e.activation(out=lnr[:, gs, 4:5], in_=lnr[:, gs, 0:1], func=AF.Sqrt),
                       [lnr[:, gs, 0:1]], [lnr[:, gs, 4:5]])
            elif stage == 1:
                gs, s_ = gvst[b]
                rstd = lnr[:, gs, 4:5]
                mean = stat[:, s_, 24:25]
                nmr = stat[:, s_, 28:29]
                P.op("dve", lambda e: e.reciprocal(out=rstd, in_=rstd), reads=[rstd], writes=[rstd])
                P.op("dve", lambda e: e.tensor_scalar(out=nmr, in0=mean, scalar1=rstd, scalar2=-1.0,
                                                      op0=ALU.mult, op1=ALU.mult),
                     reads=[mean, rstd], writes=[nmr])
                act_op(lambda e: e.activation(out=z, in_=z, func=AF.Identity, scale=rstd, bias=nmr),
                       [z, rstd, nmr], [z])
            else:
                P.op("dve", lambda e: e.tensor_tensor(out=z, in0=z, in1=lnsg[:], op=ALU.mult),
                     reads=[z, lnsg[:]], writes=[z])
                P.op("dve", lambda e: e.tensor_tensor(out=gv[:, b, :], in0=z, in1=lnsb[:], op=ALU.add),
                     reads=[z, lnsb[:]], writes=[gv[:, b, :]])

        def attn_scores(t, i, j):
            hk, b = i // 4, i % 4
            kb = b + j
            sbk = 4 + rot("sbank", 2)
            qsl = QU[:, hk * 4:(hk + 1) * 4, b * 128:(b + 1) * 128]
            ksl = kT[:, hk, kb * 128:(kb + 1) * 128]
            so = ps[:, sbk, :].rearrange("p (a b) -> p a b", b=128)
            P.op("pe", lambda e: e.matmul(so, lhsT=ksl, rhs=qsl, start=True, stop=True),
                 reads=[ksl, qsl], writes=[ps[:, sbk, :]])
            es_ = rot("ein", 3)

            def f(e):
                for hh in range(4):
                    h = hk * 4 + hh
                    ins = e.scalar_tensor_tensor(out=ein[:, es_, hh * 128:(hh + 1) * 128],
                                                 in0=distm[:, j, :], scalar=nslope[:, h:h + 1],
                                                 in1=ps[:, sbk, hh * 128:(hh + 1) * 128],
                                                 op0=ALU.mult, op1=ALU.add)
                return ins
            P.op("dve", f, reads=[distm[:], nslope[:], ps[:, sbk, :]], writes=[ein[:, es_, :]])
            pslot = (i % 2) * 3 + j
            if kb == 0:
                bias = hb[:, 2 * t:2 * t + 1]
            elif kb == 5:
                bias = hb[:, 2 * t + 1:2 * t + 2]
            else:
                bias = None

            def f(e):
                if bias is None:
                    return e.activation(out=pT[:, pslot, :], in_=ein[:, es_, :], func=AF.Exp)
                return e.activation(out=pT[:, pslot, :], in_=ein[:, es_, :], func=AF.Exp, bias=bias)
            P.op("act", f, reads=[ein[:, es_, :]] + ([bias] if bias is not None else []),
                 writes=[pT[:, pslot, :]])

        def attn_pv(t, i):
            hk, b = i // 4, i % 4
            ob, db = 6, 7

            def f(e):
                for j in range(3):
                    pslot = (i % 2) * 3 + j
                    e.matmul(ps[:, ob, :], lhsT=vv[:, b + j, hk * 128:(hk + 1) * 128], rhs=pT[:, pslot, :],
                             start=(j == 0), stop=(j == 2))
                for j in range(3):
                    pslot = (i % 2) * 3 + j
                    ins = e.matmul(ps[:, db, :], lhsT=ones[:], rhs=pT[:, pslot, :], start=(j == 0), stop=(j == 2))
                return ins
            psl = pT[:, (i % 2) * 3:(i % 2) * 3 + 3, :]
            P.op("pe", f, reads=[vv[:, b:b + 3, hk * 128:(hk + 1) * 128], psl, ones[:]],
                 writes=[ps[:, ob, :], ps[:, db, :]])
            rs_ = i % 2

            def f(e):
                for hh in range(4):
                    h = hk * 4 + hh
                    ins = e.activation(out=rden[:, rs_, hh * 128:(hh + 1) * 128],
                                       in_=ps[:, db, hh * 128:(hh + 1) * 128], func=AF.Ln,
                                       bias=esk[:, h:h + 1])
                return ins
            P.op("act", f, reads=[ps[:, db, :], esk[:]], writes=[rden[:, rs_, :]])
            P.op("act", lambda e: e.activation(out=rden[:, rs_, :], in_=rden[:, rs_, :], func=AF.Exp, scale=-1.0),
                 reads=[rden[:, rs_, :]], writes=[rden[:, rs_, :]])

        def attn_norm(t, i):
            hk, b = i // 4, i % 4
            ob, rs_ = 6, i % 2
            dst = mixA[:, hk * 4:(hk + 1) * 4, b * 128:(b + 1) * 128]
            P.op("dve", lambda e: e.tensor_tensor(
                out=dst, in0=ps[:, ob, :].rearrange("p (a b) -> p a b", b=128),
                in1=rden[:, rs_, :].rearrange("p (a b) -> p a b", b=128), op=ALU.mult),
                reads=[ps[:, ob, :], rden[:, rs_, :]], writes=[dst])

        def phase_BCD(t, base):
            seq = [("q", 4), ("q", 5), ("q", 6), ("q", 7)] + [("x", j) for j in range(16)]

            def next_piece():
                kind, j = seq.pop(0)
                if kind == "q":
                    q_piece(t, base, j, j % 4)
                else:
                    inproj_piece(t, base, j)
            pend["on"] = True
            for i in range(8):
                attn_scores(t, i, 0)
                if i >= 2:
                    attn_norm(t, i - 2)
                attn_scores(t, i, 1)
                next_piece()
                attn_scores(t, i, 2)
                next_piece()
                if i < 4:
                    next_piece()
                if i >= 1:
                    attn_pv(t, i - 1)
                act_flush()
                for b in range(NBLK):
                    sidx = i - 3 - b
                    if 0 <= sidx <= 2:
                        gv_ln_stage(b, sidx)
                act_flush()
            pend["on"] = False
            assert not seq
            attn_norm(t, 6)
            attn_pv(t, 7)
            attn_norm(t, 7)
            gv_ln_stage(3, 2)
            st["bank"] = 0
            for b in range(NBLK):
                p0 = bank2()

                def f(e, b=b, p0=p0):
                    for g in range(8):
                        ins = e.matmul(ps[:, p0 + g // 4, (g % 4) * 128:(g % 4 + 1) * 128],
                                       lhsT=gv[:, b, g * 128:(g + 1) * 128], rhs=wsT[:, g, :],
                                       start=True, stop=True)
                    return ins
                P.op("pe", f, reads=[gv[:, b, :], wsT[:]], writes=[ps[:, p0, :], ps[:, p0 + 1, :]])
                mx = ps[:, p0:p0 + 2, :].rearrange("p a b -> p (a b)")
                dst = xTm[:, 8:16, b * 128:(b + 1) * 128]
                usl = QU[:, 8:16, b * 128:(b + 1) * 128]
                z = zg(b)
                P.op("dve", lambda e, z=z, mx=mx: e.tensor_tensor(out=z, in0=mx, in1=bsb[:], op=ALU.add),
                     reads=[mx, bsb[:]], writes=[z])
                P.op("dve", lambda e, z=z, dst=dst, usl=usl: e.tensor_tensor(
                    out=dst, in0=z.rearrange("p (a b) -> p a b", b=128), in1=usl, op=ALU.mult),
                    reads=[z, usl], writes=[dst])

        def ln_apply(buf, mean, rstd):
            P.op("dve", lambda e: e.tensor_scalar(out=buf, in0=buf, scalar1=mean, scalar2=rstd,
                                                  op0=ALU.subtract, op1=ALU.mult),
                 reads=[buf, mean, rstd], writes=[buf])
            P.op("dve", lambda e: e.tensor_tensor(out=buf, in0=buf, in1=lnp[:, 0, :], op=ALU.mult),
                 reads=[buf, lnp[:, 0, :]], writes=[buf])
            P.op("dve", lambda e: e.tensor_tensor(out=buf, in0=buf, in1=lnp[:, 1, :], op=ALU.add),
                 reads=[buf, lnp[:, 1, :]], writes=[buf])

        def load_lnp(src):
            for i in range(2):
                P.op("sp", lambda e, i=i: e.dma_start(out=lnp[:, i, :], in_=src[:, i * D:(i + 1) * D]),
                     reads=[src[:, i * D:(i + 1) * D]], writes=[lnp[:, i, :]], dma_key=("lnp", i))

        def phase_E(t, base):
            load_lnp(ln1_d)
            for b in range(NBLK):
                src = xin[t, 1 + b]
                P.op("sp", lambda e, b=b, src=src: e.dma_start(out=x1[:, b, :], in_=src),
                     reads=[src], writes=[x1[:, b, :]], dma_key=("xres", b))

            def mm_pass(bp, nh):
                b0 = bank4()
                for kp in range(8):
                    pidx = base + (bp * 2 + nh) * 8 + kp
                    pc = piece_ap(pidx).rearrange("p (a b) -> p a b", b=1024)

                    def f(e, pc=pc, kp=kp):
                        for kk in range(2):
                            k = 2 * kp + kk
                            for bl in range(2):
                                for n in range(2):
                                    ins = e.matmul(ps[:, b0 + bl * 2 + n, :], lhsT=mixT_kb(k, 2 * bp + bl),
                                                   rhs=pc[:, kk, n * 512:(n + 1) * 512],
                                                   start=(k == 0), stop=(k == KC - 1))
                        return ins
                    bsl = slice(2 * bp * 128, (2 * bp + 2) * 128)
                    msrc = mixA[:, 2 * kp:2 * kp + 2, bsl] if kp < 4 else xTm[:, 2 * kp:2 * kp + 2, bsl]
                    P.op("pe", f, reads=[pc, msrc], writes=[ps[:, b0 + i, :] for i in range(4)])
                    done(pidx)
                for bl in range(2):
                    for n in range(2):
                        c0 = nh * 1024 + n * 512
                        sl = x1[:, 2 * bp + bl, c0:c0 + 512]
                        bk = b0 + bl * 2 + n
                        P.op("dve", lambda e, sl=sl, bk=bk: e.scalar_tensor_tensor(
                            out=sl, in0=sl, scalar=ALPHA, in1=ps[:, bk, :], op0=ALU.mult, op1=ALU.add),
                            reads=[sl, ps[:, bk, :]], writes=[sl])

            def ln1_pair(bp):
                blocks = [2 * bp, 2 * bp + 1]
                mr = ln_group([x1[:, b, :] for b in blocks], D)
                for i, b in enumerate(blocks):
                    ln_apply(x1[:, b, :], mr[i][0], mr[i][1])

            def xpose_pair(bp):
                for b in (2 * bp, 2 * bp + 1):
                    for q in range(4):
                        bk = bank()

                        def f(e, b=b, q=q, bk=bk):
                            for j in range(4):
                                k = q * 4 + j
                                ins = e.transpose(out=ps[:, bk, j * 128:(j + 1) * 128],
                                                  in_=x1[:, b, k * 128:(k + 1) * 128], identity=ident[:])
                            return ins
                        P.op("pe", f, reads=[x1[:, b, q * 512:(q + 1) * 512], ident[:]], writes=[ps[:, bk, :]])
                        dst = QU[:, q * 4:q * 4 + 4, b * 128:(b + 1) * 128]
                        src = ps[:, bk, :].rearrange("p (a b) -> p a b", b=128)
                        P.op("act", lambda e, dst=dst, src=src: e.activation(out=dst, in_=src, func=AF.Copy),
                             reads=[src], writes=[dst])

            mm_pass(0, 0)
            mm_pass(0, 1)
            ln1_pair(0)
            mm_pass(1, 0)
            mm_pass(1, 1)
            xpose_pair(0)
            ln1_pair(1)
            xpose_pair(1)

        def ln2_store(t, blocks):
            mr = ln_group([x1[:, b, :] for b in blocks], D)
            for i, b in enumerate(blocks):
                buf = x1[:, b, :]
                ln_apply(buf, mr[i][0], mr[i][1])
                dst = yc[t * TT + b * 128:t * TT + (b + 1) * 128, :]
                P.op("pool", lambda e, dst=dst, buf=buf: e.dma_start(out=dst, in_=buf),
                     reads=[buf], writes=[dst], dma_key=("out", b))

        def phase_F(t, base, a0gen):
            last_tile = (t == ntiles - 1)
            for kind, g in ffn_order:
                pb = base + ffn_base[(kind, g)] - ffn_base[("gu", 0)]
                if kind == "gu":
                    for which in range(2):
                        b0 = bank4()
                        for kq in range(4):
                            pc = piece_ap(pb + which * 4 + kq).rearrange("p (a b) -> p a b", b=512)

                            def f(e, pc=pc, kq=kq, b0=b0):
                                for kk in range(4):
                                    k = 4 * kq + kk
                                    for m in range(4):
                                        ins = e.matmul(ps[:, b0 + m, :], lhsT=pc[:, kk, m * 128:(m + 1) * 128],
                                                       rhs=QU[:, k, :], start=(k == 0), stop=(k == KC - 1))
                                return ins
                            P.op("pe", f, reads=[pc, QU[:, 4 * kq:4 * kq + 4, :]],
                                 writes=[ps[:, b0 + m, :] for m in range(4)])
                            done(pb + which * 4 + kq)
                        for m in range(4):
                            sgm = sg[:, m, :] if m < 2 else ein[:, m - 2, :]
                            if which == 0:
                                P.op("act", lambda e, sgm=sgm, b0=b0, m=m: e.activation(
                                    out=sgm, in_=ps[:, b0 + m, :], func=AF.Silu),
                                    reads=[ps[:, b0 + m, :]], writes=[sgm])
                            else:
                                dst = h1T[:, g % 2, m, :]
                                P.op("dve", lambda e, sgm=sgm, b0=b0, m=m, dst=dst: e.tensor_tensor(
                                    out=dst, in0=sgm, in1=ps[:, b0 + m, :], op=ALU.mult),
                                    reads=[sgm, ps[:, b0 + m, :]], writes=[dst])
                    if a0gen is not None:
                        next(a0gen, None)
                else:
                    pcs = [piece_ap(pb + kk) for kk in range(4)]
                    for b in range(NBLK):
                        b0 = bank4()

                        def f(e, b=b, b0=b0, pcs=pcs, g=g):
                            for kk in range(4):
                                for n in range(4):
                                    ins = e.matmul(ps[:, b0 + n, :], lhsT=h1T[:, g % 2, kk, b * 128:(b + 1) * 128],
                                                   rhs=pcs[kk][:, n * 512:(n + 1) * 512],
                                                   start=(kk == 0), stop=(kk == 3))
                            return ins
                        P.op("pe", f, reads=pcs + [h1T[:, g % 2, :, b * 128:(b + 1) * 128]],
                             writes=[ps[:, b0 + n, :] for n in range(4)])
                        for n in range(4):
                            sl = x1[:, b, n * 512:(n + 1) * 512]
                            if g == 0:
                                P.op("dve", lambda e, sl=sl, b0=b0, n=n: e.scalar_tensor_tensor(
                                    out=sl, in0=sl, scalar=ALPHA, in1=ps[:, b0 + n, :], op0=ALU.mult, op1=ALU.add),
                                    reads=[sl, ps[:, b0 + n, :]], writes=[sl])
                            else:
                                P.op("dve", lambda e, sl=sl, b0=b0, n=n: e.tensor_tensor(
                                    out=sl, in0=sl, in1=ps[:, b0 + n, :], op=ALU.add),
                                    reads=[sl, ps[:, b0 + n, :]], writes=[sl])
                        if last_tile and g == NG - 1:
                            ln2_store(t, [b])
                    done(pb + 3)
                    if g == NG - 3:
                        load_lnp(ln2_d)
            if a0gen is not None:
                for _ in a0gen:
                    pass
            if not last_tile:
                ln2_store(t, list(range(NBLK)))

        def tap(name, ap_sb, shape, dt):
            if name not in dbg:
                return
            dt_ = nc.dram_tensor("dbg_" + name, list(shape), dt, kind="ExternalOutput").ap()
            taps[name] = dt_
            P.op("sp", lambda e: e.dma_start(out=dt_, in_=ap_sb), reads=[ap_sb], writes=[dt_],
                 dma_key=("dbg", name))
            P.final_keys.append(("dbg", name))

        emit_consts()
        for _ in a0_steps(0):
            pass
        for t in range(ntiles):
            base = t * NPIECE
            phase_A(t, base)
            if t == 0:
                tap("QU_A", QU[:], [128, KC, TT], BF16)
                tap("kT", kT[:], [128, 2, 768], BF16)
                tap("vv", vv[:], [128, 6, 256], BF16)
                tap("gv", gv[:], [128, NBLK, 1024], BF16)
            if stop_after == "A":
                break
            phase_BCD(t, base)
            if t == 0:
                tap("mixT", xTm[:], [128, KC, TT], BF16)
                tap("mixA", mixA, [128, 8, TT], BF16)
                tap("gvD", gv[:], [128, NBLK, 1024], BF16)
                tap("QUD", QU[:], [128, KC, TT], BF16)
            if stop_after == "D":
                break
            phase_E(t, base + 28)
            if t == 0:
                tap("x1", x1[:], [128, NBLK, D], F32)
                tap("x1T", QU[:], [128, KC, TT], BF16)
            if stop_after == "E":
                break
            gen = a0_steps(t + 1) if t + 1 < ntiles else None
            phase_F(t, base + 60, gen)
        if stop_after == "F":
            for b in range(NBLK):
                P.final_keys.append(("out", b))

        keys = [("eng", e) for e in Prog.ENGS] + [("dma", k) for k in P.dma_keys()]
        sems = {}
        for i, k in enumerate(keys):
            sems[k] = es.enter_context(nc.semaphore("s%d" % i))
        run = P.emit(nc, sems)
        block = es.enter_context(nc.Block())

        @block.tensor
        def _(e):
            run("pe", e)

        @block.scalar
        def _(e):
            run("act", e)

        @block.vector
        def _(e):
            run("dve", e)

        @block.gpsimd
        def _(e):
            run("pool", e)

        @block.sync
        def _(e):
            run("sp", e)
    nc._prog_stats = {"nops": len(P.ops), "counts": P.counts, "nsems": len(keys)}
    return nc


def _alibi_consts():
    kk = np.arange(128)[:, None]
    qq = np.arange(128)[None, :]
    dm = np.zeros((128, 3, 128), np.float32)
    for j in range(3):
        dist = np.abs(kk + (j - 1) * 128 - qq)
        dm[:, j, :] = np.where(dist <= 128, dist, 1.0e6)
    h = np.arange(1, 9, dtype=np.float32)
    slopes = (2.0 ** (-8.0 * h / 8)).astype(np.float32)
    nsl = np.broadcast_to(-slopes[None, :], (128, 8)).astype(np.float32)
    return dm.reshape(128, 384), np.ascontiguousarray(nsl)


def _rep(v, n=128):
    v = np.asarray(v, np.float32).reshape(1, -1)
    return np.ascontiguousarray(np.broadcast_to(v, (n, v.shape[1])))


def make_in_maps(x_prompt, x_sample, w_in, ln_sgu_g, ln_sgu_b, w_s, b_s, attn_sink, w_o,
                 ln1_g, ln1_b, w_gate, w_up, w_down, ln2_g, ln2_b):
    f = lambda a: np.ascontiguousarray(np.asarray(a, np.float32))
    xp, xsm = f(x_prompt), f(x_sample)
    dm, nsl = _alibi_consts()
    shared = {
        "w_in": f(w_in[0]), "w_o": f(w_o[0]), "w_gate": f(w_gate[0]), "w_up": f(w_up[0]),
        "w_down": f(w_down[0]),
        "distm": dm, "nslope": nsl,
        "sink": _rep(attn_sink[0]),
        "bsb": _rep(np.asarray(b_s[0]).reshape(-1)),
        "lnsg": _rep(ln_sgu_g[0]), "lnsb": _rep(ln_sgu_b[0]),
        "ln1": np.concatenate([_rep(ln1_g[0]), _rep(ln1_b[0])], axis=1),
        "ln2": np.concatenate([_rep(ln2_g[0]), _rep(ln2_b[0])], axis=1),
        "wst": f(np.transpose(np.asarray(w_s[0], np.float32), (2, 0, 1)).reshape(128, 1024)),
        "ident": np.eye(128, dtype=np.float32),
    }
    in_maps = []
    for c in range(8):
        segs = [(xp[c], 0, 2048), (xsm[c // 2], (c % 2) * 1024, (c % 2) * 1024 + 1024)]
        xin = np.zeros((NTILE, 6, 128, D), np.float32)
        hbv = np.zeros((12,), np.float32)
        t = 0
        for seq, lo, hi in segs:
            for s0 in range(lo, hi, TT):
                xin[t, 1:5] = seq[s0:s0 + TT].reshape(4, 128, D)
                if s0 - 128 >= 0:
                    xin[t, 0] = seq[s0 - 128:s0]
                else:
                    hbv[2 * t] = NEGB
                if s0 + TT + 128 <= seq.shape[0]:
                    xin[t, 5] = seq[s0 + TT:s0 + TT + 128]
                else:
                    hbv[2 * t + 1] = NEGB
                t += 1
        m = dict(shared)
        m["xin"] = xin
        m["hb"] = _rep(hbv)
        in_maps.append(m)
    return in_maps


def gather_outputs(results):
    yp = np.zeros((8, 2048, D), np.float32)
    ysm = np.zeros((4, 2048, D), np.float32)
    for c in range(8):
        y = np.asarray(results[c]["yc"], np.float32)
        yp[c] = y[0:2048]
        ysm[c // 2, (c % 2) * 1024:(c % 2) * 1024 + 1024] = y[2048:3072]
    return yp, ysm


_NC_CACHE = {}


def kernel(x_prompt, x_sample, w_in, ln_sgu_g, ln_sgu_b, w_s, b_s, attn_sink, w_o,
           ln1_g, ln1_b, w_gate, w_up, w_down, ln2_g, ln2_b):
    in_maps = make_in_maps(x_prompt, x_sample, w_in, ln_sgu_g, ln_sgu_b, w_s, b_s, attn_sink, w_o,
                           ln1_g, ln1_b, w_gate, w_up, w_down, ln2_g, ln2_b)
    nc = build()
    res = run_bass_kernel_spmd(nc, in_maps, core_ids=list(range(8)))
    return gather_outputs(res.results)
```

```python
import math
import numpy as np
import concourse.bass as bass
import concourse.mybir as mybir
from concourse.bass_utils import run_bass_kernel_spmd

F32 = mybir.dt.float32
BF16 = mybir.dt.bfloat16
AF = mybir.ActivationFunctionType
ALU = mybir.AluOpType

D = 2048
KC = 16
NBLK = 4
TT = 512
NTILE = 6
NTOK = NTILE * TT
DFF = 5632
NG = 11
RS = 11
PIECE = 2048
NPIECE = 28 + 32 + 12 * NG
CONV_BATCH = 8
CAST_TILES = 2
NEGB = -1.0e4
ALPHA = float((2.0 * 1) ** 0.25)
QSCALE = 1.0 / math.sqrt(128.0)
LN_EPS = 1e-5


class _Rec:
    __slots__ = ("ivs", "lo", "hi", "w", "rs")

    def __init__(self, ivs):
        self.ivs = ivs
        self.lo = min(a for a, _ in ivs)
        self.hi = max(b for _, b in ivs)
        self.w = None
        self.rs = []


class _Op:
    __slots__ = ("eng", "fn", "reads", "writes", "idx", "deps", "signal", "semval", "dma_key",
                 "dma_val", "batch")

    def __init__(self, eng, fn, reads, writes, dma_key):
        self.eng = eng
        self.fn = fn
        self.reads = reads
        self.writes = writes
        self.deps = {}
        self.signal = False
        self.semval = None
        self.dma_key = dma_key
        self.dma_val = None
        self.batch = None


def _intervals(ap):
    dsz = mybir.dt.size(ap.dtype)
    space = str(ap.space)
    dims = list(ap.ap)
    off = ap.offset
    if space != "DRAM":
        pstride = dims[0][0]
        if pstride > 0:
            off = off % pstride
        dims = dims[1:]
    region = (space, ap.tensor.name)
    dims = [(abs(s), c) for s, c in dims if c > 1]
    if not dims:
        ivs = [(off * dsz, (off + 1) * dsz)]
    else:
        dims.sort(key=lambda sc: sc[0])
        s0, c0 = dims[0]
        run = (c0 - 1) * s0 + 1
        outer = dims[1:]
        n_outer = 1
        for _, c in outer:
            n_outer *= c
        if n_outer <= 32:
            starts = [off]
            for s, c in outer:
                starts = [st + i * s for st in starts for i in range(c)]
            ivs = [(st * dsz, (st + run) * dsz) for st in starts]
        else:
            ext = run + sum((c - 1) * s for s, c in outer)
            ivs = [(off * dsz, (off + ext) * dsz)]
    if space == "PSUM":
        lo = min(a for a, _ in ivs) // 2048 * 2048
        hi = (max(b for _, b in ivs) + 2047) // 2048 * 2048
        ivs = [(lo, hi)]
    return region, tuple(ivs)


def _overlap(r, ivs, lo, hi):
    if r.hi <= lo or hi <= r.lo:
        return False
    for a, b in r.ivs:
        for c, d in ivs:
            if a < d and c < b:
                return True
    return False


class Prog:
    ENGS = ("pe", "act", "dve", "pool", "sp")

    def __init__(self):
        self.ops = []
        self.regions = {}
        self.cur_batch = None
        self.final_keys = []

    def op(self, eng, fn, reads=(), writes=(), dma_key=None):
        o = _Op(eng, fn, list(reads), list(writes), dma_key)
        o.idx = len(self.ops)
        if dma_key is not None and self.cur_batch is not None:
            o.batch = self.cur_batch
            self.cur_batch.append(o)
        self.ops.append(o)
        psum_reads = []
        for ap in o.reads:
            region, ivs = _intervals(ap)
            if region[0] == "PSUM":
                psum_reads.append((region, ivs))
                continue
            self._access(o, region, ivs, False)
        for ap in o.writes:
            region, ivs = _intervals(ap)
            self._access(o, region, ivs, True)
        for region, ivs in psum_reads:
            self._access(o, region, ivs, True, kind_raw=True)
        return o

    def _access(self, o, region, ivs, is_write, kind_raw=False):
        recs = self.regions.setdefault(region, {})
        lo = min(a for a, _ in ivs)
        hi = max(b for _, b in ivs)
        mine = recs.get(ivs)
        if mine is None:
            mine = _Rec(ivs)
            recs[ivs] = mine
        for r in recs.values():
            if r is not mine and not _overlap(r, ivs, lo, hi):
                continue
            if r.w is not None and r.w is not o:
                k = "raw" if (not is_write or kind_raw) else "waw"
                self._dep(o, r.w, k)
            if is_write:
                for rd in r.rs:
                    if rd is not o:
                        self._dep(o, rd, "war")
        if is_write:
            for r in list(recs.values()):
                if r is mine or _overlap(r, ivs, lo, hi):
                    r.rs = []
                    r.w = o
        else:
            mine.rs.append(o)

    @staticmethod
    def _dep(o, d, kind):
        prev = o.deps.get(d)
        if prev is None or kind == "raw":
            o.deps[d] = kind

    def batch_begin(self):
        self.cur_batch = []

    def batch_end(self):
        self.cur_batch = None

    def emit(self, nc, sems):
        ops = self.ops
        for o in ops:
            keep = {}
            for d, kind in o.deps.items():
                if d.dma_key is None and o.dma_key is None and d.eng == o.eng:
                    if o.eng == "pe":
                        continue
                    if kind != "raw":
                        continue
                keep[d] = kind
            o.deps = keep
            for d in keep:
                d.signal = True
        cnt = {e: 0 for e in self.ENGS}
        dcnt = {}
        for o in ops:
            if o.dma_key is not None:
                dcnt[o.dma_key] = dcnt.get(o.dma_key, 0) + 16
                o.dma_val = dcnt[o.dma_key]
            elif o.signal:
                cnt[o.eng] += 1
                o.semval = cnt[o.eng]
        for o in ops:
            if o.batch is not None:
                o.dma_val = max(m.dma_val for m in o.batch)
        self.final_dma = dict(dcnt)
        self.counts = cnt
        streams = {e: [o for o in ops if o.eng == e] for e in self.ENGS}

        def run(eng_name, e):
            seen = {}
            for o in streams[eng_name]:
                waits = {}
                for d in o.deps:
                    if d.dma_key is not None:
                        key, val = ("dma", d.dma_key), d.dma_val
                    else:
                        key, val = ("eng", d.eng), d.semval
                    if waits.get(key, 0) < val:
                        waits[key] = val
                for key, val in waits.items():
                    if seen.get(key, 0) < val:
                        e.wait_ge(sems[key], val)
                        seen[key] = val
                ins = o.fn(e)
                if o.dma_key is not None:
                    ins.then_inc(sems[("dma", o.dma_key)], 16)
                elif o.signal:
                    ins.then_inc(sems[("eng", eng_name)], 1)
            if eng_name == "sp":
                for key in self.final_keys:
                    e.wait_ge(sems[("dma", key)], self.final_dma[key])
        return run

    def dma_keys(self):
        return sorted({o.dma_key for o in self.ops if o.dma_key is not None}, key=str)


def build(ntiles=NTILE, stop_after="F", dbg=None):
    dbg = dbg or []
    nc = bass.Bass("TRN2", target_bir_lowering=False)
    P = Prog()

    def din(name, shape, dt=F32):
        return nc.dram_tensor(name, list(shape), dt, kind="ExternalInput").ap()

    xin = din("xin", [NTILE, 6, 128, D])
    hb_d = din("hb", [128, 12])
    w_in = din("w_in", [D, 3584])
    w_o = din("w_o", [D, D])
    w_g = din("w_gate", [D, DFF])
    w_u = din("w_up", [D, DFF])
    w_d = din("w_down", [DFF, D])
    distm_d = din("distm", [128, 3 * 128])
    nslope_d = din("nslope", [128, 8])
    sink_d = din("sink", [128, 8])
    bsb_d = din("bsb", [128, 1024])
    lnsg_d = din("lnsg", [128, 1024])
    lnsb_d = din("lnsb", [128, 1024])
    ln1_d = din("ln1", [128, 2 * D])
    ln2_d = din("ln2", [128, 2 * D])
    wst_d = din("wst", [128, 1024])
    ident_d = din("ident", [128, 128])
    yc = nc.dram_tensor("yc", [NTOK, D], F32, kind="ExternalOutput").ap()
    wsc = nc.dram_tensor("wsc", [NPIECE, 128, PIECE], BF16, kind="Internal").ap()
    taps = {}

    import contextlib
    es = contextlib.ExitStack()
    with es:
        def sb(name, shape, dt):
            return es.enter_context(nc.sbuf_tensor("sb_" + name, list(shape), dt))

        ring = sb("ring", [128, RS, PIECE], BF16)
        xs = sb("xs", [128, 2, 1024], F32)
        xTm = sb("xTm", [128, KC, TT], BF16)
        xTh = sb("xTh", [128, KC, 256], BF16)
        QU = sb("QU", [128, KC, TT], BF16)
        kT = sb("kT", [128, 2, 768], BF16)
        vv = sb("vv", [128, 6, 256], BF16)
        gv = sb("gv", [128, NBLK, 1024], BF16)
        tmp4 = sb("tmp4", [128, 2, 1024], F32)
        ein = sb("ein", [128, 3, 512], F32)
        pT = sb("pT", [128, 6, 512], BF16)
        rden = sb("rden", [128, 2, 512], F32)
        x1 = sb("x1", [128, NBLK, D], F32)
        sg = sb("sg", [128, 2, 512], F32)
        h1T = sb("h1T", [128, 2, 4, TT], BF16)
        lnp = sb("lnp", [128, 2, D], F32)
        distm = sb("distm", [128, 3, 128], F32)
        nslope = sb("nslope", [128, 8], F32)
        esk = sb("esk", [128, 8], F32)
        hb = sb("hbias", [128, 12], F32)
        bsb = sb("bsb", [128, 1024], F32)
        lnsg = sb("lnsg", [128, 1024], F32)
        lnsb = sb("lnsb", [128, 1024], F32)
        wsT = sb("wsT", [128, 8, 128], BF16)
        ident = sb("ident", [128, 128], F32)
        ones = sb("ones", [128, 128], BF16)
        stat = sb("stat", [128, 8, 32], F32)
        lnr = sb("lnr", [128, 4, 8], F32)
        ps = es.enter_context(nc.psum_tensor("ps", [128, 8, 512], F32))

        st = {"bank": 0, "stat": 0, "xs": 0, "tmp4": 0, "ein": 0, "pT": 0, "rden": 0, "sg": 0,
              "nload": 0, "cid": 0, "done": -1, "lnr": 0, "sbank": 0}

        def bank():
            b = st["bank"]
            st["bank"] = (b + 1) % 8
            return b

        def bank2():
            if st["bank"] % 2:
                st["bank"] = (st["bank"] + 1) % 8
            b = st["bank"]
            st["bank"] = (b + 2) % 8
            return b

        def bank4():
            if st["bank"] % 4:
                st["bank"] = (st["bank"] + 4 - st["bank"] % 4) % 8
            b = st["bank"]
            st["bank"] = (b + 4) % 8
            return b

        def rot(name, n):
            v = st[name]
            st[name] = (v + 1) % n
            return v

        def cid():
            st["cid"] += 1
            return st["cid"]

        win_r = w_in.rearrange("(k p) n -> p k n", p=128)
        wo_r = w_o.rearrange("(k p) n -> p k n", p=128)
        wg_r = w_g.rearrange("(k p) n -> p k n", p=128)
        wu_r = w_u.rearrange("(k p) n -> p k n", p=128)
        pieces = []

        def colpiece(wr, c0):
            pieces.append((wr[:, :, c0:c0 + 128], 16, 128))

        for m in range(2):
            colpiece(win_r, 1024 + m * 128)
        for j in range(2):
            pieces.append((win_r[:, j * 8:(j + 1) * 8, 1280:1536], 8, 256))
        for m in range(8):
            colpiece(win_r, m * 128)
        for n in range(2):
            for kq in range(4):
                pieces.append((win_r[:, 4 * kq:4 * kq + 4, 2560 + n * 512:2560 + (n + 1) * 512], 4, 512))
        for m in range(8):
            colpiece(win_r, 1536 + m * 128)
        for bp in range(2):
            for nh in range(2):
                for kp in range(8):
                    pieces.append((wo_r[:, 2 * kp:2 * kp + 2, nh * 1024:(nh + 1) * 1024], 2, 1024))
        ffn_order = []
        for g in range(NG):
            ffn_order.append(("gu", g))
            if g >= 1:
                ffn_order.append(("d", g - 1))
        ffn_order.append(("d", NG - 1))
        ffn_base = {}
        for kind, g in ffn_order:
            ffn_base[(kind, g)] = len(pieces)
            if kind == "gu":
                for wr in (wg_r, wu_r):
                    for kq in range(4):
                        pieces.append((wr[:, 4 * kq:4 * kq + 4, g * 512:(g + 1) * 512], 4, 512))
            else:
                for kk in range(4):
                    c = g * 4 + kk
                    pieces.append((w_d[c * 128:(c + 1) * 128, :], 1, 2048))
        assert len(pieces) == NPIECE

        def emit_conversion():
            for i0 in range(0, NPIECE, CONV_BATCH):
                P.batch_begin()
                for i in range(i0, min(NPIECE, i0 + CONV_BATCH)):
                    src, a, b = pieces[i]
                    if a == 1:
                        dst = wsc[i]
                    else:
                        dst = wsc[i].rearrange("p (a b) -> p a b", b=b)
                    P.op("pool", lambda e, dst=dst, src=src: e.dma_start(out=dst, in_=src),
                         reads=[src], writes=[dst], dma_key=("conv", i0 // CONV_BATCH))
                P.batch_end()

        total_pieces = ntiles * NPIECE

        def lookahead():
            lim = min(total_pieces, st["done"] + 1 + RS)
            while st["nload"] < lim:
                i = st["nload"]
                slot = i % RS
                dst = ring[:, slot, :]
                if i < NPIECE * CAST_TILES:
                    src, a, b = pieces[i % NPIECE]
                    dstv = dst if a == 1 else dst.rearrange("p (a b) -> p a b", b=b)
                    P.op("pool", lambda e, dstv=dstv, src=src: e.dma_start(out=dstv, in_=src),
                         reads=[src], writes=[dst], dma_key=("ring", slot))
                    if ntiles > CAST_TILES and (i % NPIECE) % CAST_TILES == i // NPIECE:
                        wdst = wsc[i % NPIECE]
                        P.op("sp", lambda e, wdst=wdst, dst=dst: e.dma_start(out=wdst, in_=dst),
                             reads=[dst], writes=[wdst], dma_key=("wst", slot))
                else:
                    src = wsc[i % NPIECE]
                    P.op("sp", lambda e, dst=dst, src=src: e.dma_start(out=dst, in_=src),
                         reads=[src], writes=[dst], dma_key=("ring", slot))
                st["nload"] += 1

        def piece_ap(gidx):
            assert gidx < st["done"] + 1 + RS
            lookahead()
            return ring[:, gidx % RS, :]

        def done(gidx):
            st["done"] = max(st["done"], gidx)
            lookahead()

        def cload(dst, src):
            P.op("sp", lambda e: e.dma_start(out=dst, in_=src), reads=[src], writes=[dst],
                 dma_key=("c", cid()))

        def emit_consts():
            cload(ident[:], ident_d)
            cload(hb[:], hb_d)
            cload(distm[:].rearrange("p a b -> p (a b)"), distm_d)
            cload(nslope[:], nslope_d)
            cload(esk[:], sink_d)
            cload(bsb[:], bsb_d)
            cload(lnsg[:], lnsg_d)
            cload(lnsb[:], lnsb_d)
            cload(tmp4[:, 0, :], wst_d)
            P.op("dve", lambda e: e.memset(ones[:], 1.0), writes=[ones[:]])
            P.op("dve", lambda e: e.tensor_copy(out=wsT[:].rearrange("p a b -> p (a b)"), in_=tmp4[:, 0, :]),
                 reads=[tmp4[:, 0, :]], writes=[wsT[:]])
            P.op("act", lambda e: e.activation(out=esk[:], in_=esk[:], func=AF.Exp),
                 reads=[esk[:]], writes=[esk[:]])

        def xT_dst(blk, k0, k1):
            if blk == 0:
                return xTh[:, k0:k1, 0:128]
            if blk == 5:
                return xTh[:, k0:k1, 128:256]
            return xTm[:, k0:k1, (blk - 1) * 128:blk * 128]

        def a0_steps(t):
            items = [(blk, hf) for blk in range(6) for hf in range(2)]
            if t == 0:
                items = [it for it in items if it[0] in (0, 5)] + [it for it in items if it[0] not in (0, 5)]
                for b in range(NBLK):
                    src0 = xin[0, 1 + b]
                    P.op("sp", lambda e, b=b, src0=src0: e.dma_start(out=x1[:, b, :], in_=src0),
                         reads=[src0], writes=[x1[:, b, :]], dma_key=("xres", b))
            slots = []

            def load(i):
                blk, hf = items[i]
                if t == 0 and blk not in (0, 5):
                    slots.append(None)
                    return
                s = rot("xs", 2)
                slots.append(s)
                src = xin[t, blk, :, hf * 1024:(hf + 1) * 1024]
                dst = xs[:, s, :]
                P.op("sp", lambda e: e.dma_start(out=dst, in_=src), reads=[src], writes=[dst],
                     dma_key=("xs", s))

            load(0)
            load(1)
            for i, (blk, hf) in enumerate(items):
                s = slots[i]
                for q in range(2):
                    if t == 0:
                        b = bank()
                    else:
                        b = (st["bank"] + (4 - st["bank"] % 4) % 4) % 8 + q
                    k0 = hf * 8 + q * 4

                    if s is None:
                        srcx = x1[:, blk - 1, hf * 1024 + q * 512:hf * 1024 + (q + 1) * 512]
                    else:
                        srcx = xs[:, s, q * 512:(q + 1) * 512]

                    def tfn(e, srcx=srcx, b=b):
                        for j in range(4):
                            ins = e.transpose(out=ps[:, b, j * 128:(j + 1) * 128],
                                              in_=srcx[:, j * 128:(j + 1) * 128], identity=ident[:])
                        return ins
                    P.op("pe", tfn, reads=[srcx, ident[:]], writes=[ps[:, b, :]])
                    dst = xT_dst(blk, k0, k0 + 4)
                    src = ps[:, b, :].rearrange("p (a b) -> p a b", b=128)
                    P.op("act", lambda e, dst=dst, src=src: e.activation(out=dst, in_=src, func=AF.Copy),
                         reads=[src], writes=[dst])
                if i + 2 < len(items):
                    load(i + 2)
                yield

        def xT_blk(k, blk):
            if blk == 0:
                return xTh[:, k, 0:128]
            if blk == 5:
                return xTh[:, k, 128:256]
            return xTm[:, k, (blk - 1) * 128:blk * 128]

        def ln_group(bufs, n):
            gs = rot("lnr", 4)
            nchunk = n // 512
            means = []
            for i, buf in enumerate(bufs):
                s = rot("stat", 8)
                bst = stat[:, s, 0:6 * nchunk]
                mv = stat[:, s, 24:26]

                def f1(e, s=s, buf=buf):
                    for c in range(nchunk):
                        ins = e.bn_stats(out=stat[:, s, 6 * c:6 * c + 6], in_=buf[:, c * 512:(c + 1) * 512])
                    return ins
                P.op("dve", f1, reads=[buf], writes=[bst])
                P.op("dve", lambda e, mv=mv, bst=bst: e.bn_aggr(out=mv, in_=bst), reads=[bst], writes=[mv])
                P.op("dve", lambda e, s=s, i=i: e.tensor_scalar(out=lnr[:, gs, i:i + 1], in0=stat[:, s, 25:26],
                                                               scalar1=LN_EPS, scalar2=None, op0=ALU.add),
                     reads=[stat[:, s, 25:26]], writes=[lnr[:, gs, i:i + 1]])
                means.append(stat[:, s, 24:25])
            nb = len(bufs)
            P.op("act", lambda e: e.activation(out=lnr[:, gs, 4:4 + nb], in_=lnr[:, gs, 0:nb], func=AF.Sqrt),
                 reads=[lnr[:, gs, 0:nb]], writes=[lnr[:, gs, 4:4 + nb]])
            P.op("dve", lambda e: e.reciprocal(out=lnr[:, gs, 4:4 + nb], in_=lnr[:, gs, 4:4 + nb]),
                 reads=[lnr[:, gs, 4:4 + nb]], writes=[lnr[:, gs, 4:4 + nb]])
            return [(means[i], lnr[:, gs, 4 + i:5 + i]) for i in range(nb)]

        pend = {"on": False, "ops": []}

        def bank_guard(banks):
            for fn, reads, writes in pend["ops"]:
                for r in reads:
                    if str(r.space) == "PSUM" and (r.offset % 4096) // 512 in banks:
                        act_flush()
                        return

        def act_op(fn, reads, writes):
            if pend["on"]:
                pend["ops"].append((fn, reads, writes))
            else:
                P.op("act", fn, reads=reads, writes=writes)

        def act_flush():
            for fn, reads, writes in pend["ops"]:
                P.op("act", fn, reads=reads, writes=writes)
            pend["ops"] = []

        def q_piece(t, base, m, b):
            pc = piece_ap(base + 4 + m)
            bank_guard([b])

            def f(e):
                for k in range(KC):
                    ins = e.matmul(ps[:, b, :], lhsT=pc[:, k * 128:(k + 1) * 128], rhs=xTm[:, k, :],
                                   start=(k == 0), stop=(k == KC - 1))
                return ins
            P.op("pe", f, reads=[pc, xTm[:]], writes=[ps[:, b, :]])
            done(base + 4 + m)
            act_op(lambda e: e.activation(out=QU[:, m, :], in_=ps[:, b, :], func=AF.Copy, scale=QSCALE),
                   [ps[:, b, :]], [QU[:, m, :]])

        def phase_A(t, base):
            for hk in range(2):
                pc = piece_ap(base + hk)
                b1 = bank()
                b2 = bank()

                def f(e, pc=pc, b1=b1, b2=b2):
                    for k in range(KC):
                        e.matmul(ps[:, b1, :], lhsT=pc[:, k * 128:(k + 1) * 128], rhs=xTm[:, k, :],
                                 start=(k == 0), stop=(k == KC - 1))
                        ins = e.matmul(ps[:, b2, 0:256], lhsT=pc[:, k * 128:(k + 1) * 128], rhs=xTh[:, k, :],
                                       start=(k == 0), stop=(k == KC - 1))
                    return ins
                P.op("pe", f, reads=[pc, xTm[:], xTh[:]], writes=[ps[:, b1, :], ps[:, b2, :]])
                done(base + hk)
                P.op("act", lambda e, b1=b1, hk=hk: e.activation(out=kT[:, hk, 128:640], in_=ps[:, b1, :],
                                                                func=AF.Copy),
                     reads=[ps[:, b1, :]], writes=[kT[:, hk, 128:640]])
                dst = kT[:, hk, :].rearrange("p (a b) -> p a b", b=128)[:, 0:6:5, :]
                src = ps[:, b2, 0:256].rearrange("p (a b) -> p a b", b=128)
                P.op("act", lambda e, dst=dst, src=src: e.activation(out=dst, in_=src, func=AF.Copy),
                     reads=[src], writes=[dst])
            vb = [bank() for _ in range(6)]
            for j in range(2):
                pc = piece_ap(base + 2 + j).rearrange("p (a b) -> p a b", b=256)

                def f(e, pc=pc, j=j):
                    for kq in range(8):
                        k = j * 8 + kq
                        for blk in range(6):
                            ins = e.matmul(ps[:, vb[blk], 0:256], lhsT=xT_blk(k, blk), rhs=pc[:, kq, :],
                                           start=(k == 0), stop=(k == KC - 1))
                    return ins
                P.op("pe", f, reads=[pc, xTm[:], xTh[:]], writes=[ps[:, vb[blk], :] for blk in range(6)])
                done(base + 2 + j)
            for blk in range(6):
                P.op("act", lambda e, blk=blk: e.activation(out=vv[:, blk, :], in_=ps[:, vb[blk], 0:256],
                                                           func=AF.Copy),
                     reads=[ps[:, vb[blk], 0:256]], writes=[vv[:, blk, :]])
            for m in range(4):
                q_piece(t, base, m, bank())

        zg_alias = h1T[:].rearrange("p a b c -> p (a b c)").bitcast(F32).rearrange("p (a b) -> p a b", b=1024)

        def zg(b):
            return tmp4[:, b, :] if b < 2 else zg_alias[:, b - 2, :]

        mixA = xs[:].rearrange("p a b -> p (a b)").bitcast(BF16).rearrange("p (a b) -> p a b", b=512)

        def mixT_kb(k, b):
            if k < 8:
                return mixA[:, k, b * 128:(b + 1) * 128]
            return xTm[:, k, b * 128:(b + 1) * 128]

        def inproj_piece(t, base, idx):
            if idx < 8:
                n, kq = idx // 4, idx % 4
                pc = piece_ap(base + 12 + idx).rearrange("p (a b) -> p a b", b=512)
                if kq == 0:
                    bank_guard([0, 1, 2, 3])

                def f(e, pc=pc, kq=kq):
                    for kk in range(4):
                        k = 4 * kq + kk
                        for b in range(NBLK):
                            ins = e.matmul(ps[:, b, :], lhsT=xTm[:, k, b * 128:(b + 1) * 128], rhs=pc[:, kk, :],
                                           start=(k == 0), stop=(k == KC - 1))
                    return ins
                P.op("pe", f, reads=[pc, xTm[:, 4 * kq:4 * kq + 4, :]], writes=[ps[:, i, :] for i in range(4)])
                done(base + 12 + idx)
                if kq == 3:
                    for b in range(NBLK):
                        act_op(lambda e, b=b, n=n: e.activation(out=zg(b)[:, n * 512:(n + 1) * 512],
                                                               in_=ps[:, b, :], func=AF.Gelu_apprx_tanh),
                               [ps[:, b, :]], [zg(b)[:, n * 512:(n + 1) * 512]])
            else:
                m = idx - 8
                pc = piece_ap(base + 12 + idx)
                b = m % 4
                bank_guard([b])

                def f(e, pc=pc, b=b):
                    for k in range(KC):
                        ins = e.matmul(ps[:, b, :], lhsT=pc[:, k * 128:(k + 1) * 128], rhs=xTm[:, k, :],
                                       start=(k == 0), stop=(k == KC - 1))
                    return ins
                P.op("pe", f, reads=[pc, xTm[:]], writes=[ps[:, b, :]])
                done(base + 12 + idx)
                act_op(lambda e, b=b, m=m: e.activation(out=QU[:, 8 + m, :], in_=ps[:, b, :],
                                                       func=AF.Gelu_apprx_tanh),
                       [ps[:, b, :]], [QU[:, 8 + m, :]])

        gvst = {}

        def gv_ln_stage(b, stage):
            z = zg(b)
            if stage == 0:
                gs = rot("lnr", 4)
                s_ = rot("stat", 8)
                gvst[b] = (gs, s_)
                bst = stat[:, s_, 0:12]
                mv = stat[:, s_, 24:26]

                def f1(e):
                    e.bn_stats(out=stat[:, s_, 0:6], in_=z[:, 0:512])
                    return e.bn_stats(out=stat[:, s_, 6:12], in_=z[:, 512:1024])
                P.op("dve", f1, reads=[z], writes=[bst])
                P.op("dve", lambda e: e.bn_aggr(out=mv, in_=bst), reads=[bst], writes=[mv])
                P.op("dve", lambda e: e.tensor_scalar(out=lnr[:, gs, 0:1], in0=stat[:, s_, 25:26], scalar1=LN_EPS,
                                                      scalar2=None, op0=ALU.add),
                     reads=[stat[:, s_, 25:26]], writes=[lnr[:, gs, 0:1]])
                act_op(lambda e: e.activation(out=lnr[:, gs, 4:5], in_=lnr[:, gs, 0:1], func=AF.Sqrt),
                       [lnr[:, gs, 0:1]], [lnr[:, gs, 4:5]])
            elif stage == 1:
                gs, s_ = gvst[b]
                rstd = lnr[:, gs, 4:5]
                mean = stat[:, s_, 24:25]
                nmr = stat[:, s_, 28:29]
                P.op("dve", lambda e: e.reciprocal(out=rstd, in_=rstd), reads=[rstd], writes=[rstd])
                P.op("dve", lambda e: e.tensor_scalar(out=nmr, in0=mean, scalar1=rstd, scalar2=-1.0,
                                                      op0=ALU.mult, op1=ALU.mult),
                     reads=[mean, rstd], writes=[nmr])
                act_op(lambda e: e.activation(out=z, in_=z, func=AF.Identity, scale=rstd, bias=nmr),
                       [z, rstd, nmr], [z])
            else:
                P.op("dve", lambda e: e.tensor_tensor(out=z, in0=z, in1=lnsg[:], op=ALU.mult),
                     reads=[z, lnsg[:]], writes=[z])
                P.op("dve", lambda e: e.tensor_tensor(out=gv[:, b, :], in0=z, in1=lnsb[:], op=ALU.add),
                     reads=[z, lnsb[:]], writes=[gv[:, b, :]])

        def attn_scores(t, i, j):
            hk, b = i // 4, i % 4
            kb = b + j
            sbk = 4 + rot("sbank", 2)
            qsl = QU[:, hk * 4:(hk + 1) * 4, b * 128:(b + 1) * 128]
            ksl = kT[:, hk, kb * 128:(kb + 1) * 128]
            so = ps[:, sbk, :].rearrange("p (a b) -> p a b", b=128)
            P.op("pe", lambda e: e.matmul(so, lhsT=ksl, rhs=qsl, start=True, stop=True),
                 reads=[ksl, qsl], writes=[ps[:, sbk, :]])
            es_ = rot("ein", 3)

            def f(e):
                for hh in range(4):
                    h = hk * 4 + hh
                    ins = e.scalar_tensor_tensor(out=ein[:, es_, hh * 128:(hh + 1) * 128],
                                                 in0=distm[:, j, :], scalar=nslope[:, h:h + 1],
                                                 in1=ps[:, sbk, hh * 128:(hh + 1) * 128],
                                                 op0=ALU.mult, op1=ALU.add)
                return ins
            P.op("dve", f, reads=[distm[:], nslope[:], ps[:, sbk, :]], writes=[ein[:, es_, :]])
            pslot = (i % 2) * 3 + j
            if kb == 0:
                bias = hb[:, 2 * t:2 * t + 1]
            elif kb == 5:
                bias = hb[:, 2 * t + 1:2 * t + 2]
            else:
                bias = None

            def f(e):
                if bias is None:
                    return e.activation(out=pT[:, pslot, :], in_=ein[:, es_, :], func=AF.Exp)
                return e.activation(out=pT[:, pslot, :], in_=ein[:, es_, :], func=AF.Exp, bias=bias)
            P.op("act", f, reads=[ein[:, es_, :]] + ([bias] if bias is not None else []),
                 writes=[pT[:, pslot, :]])

        def attn_pv(t, i):
            hk, b = i // 4, i % 4
            ob, db = 6, 7

            def f(e):
                for j in range(3):
                    pslot = (i % 2) * 3 + j
                    e.matmul(ps[:, ob, :], lhsT=vv[:, b + j, hk * 128:(hk + 1) * 128], rhs=pT[:, pslot, :],
                             start=(j == 0), stop=(j == 2))
                for j in range(3):
                    pslot = (i % 2) * 3 + j
                    ins = e.matmul(ps[:, db, :], lhsT=ones[:], rhs=pT[:, pslot, :], start=(j == 0), stop=(j == 2))
                return ins
            psl = pT[:, (i % 2) * 3:(i % 2) * 3 + 3, :]
            P.op("pe", f, reads=[vv[:, b:b + 3, hk * 128:(hk + 1) * 128], psl, ones[:]],
                 writes=[ps[:, ob, :], ps[:, db, :]])
            rs_ = i % 2

            def f(e):
                for hh in range(4):
                    h = hk * 4 + hh
                    ins = e.activation(out=rden[:, rs_, hh * 128:(hh + 1) * 128],
                                       in_=ps[:, db, hh * 128:(hh + 1) * 128], func=AF.Ln,
                                       bias=esk[:, h:h + 1])
                return ins
            P.op("act", f, reads=[ps[:, db, :], esk[:]], writes=[rden[:, rs_, :]])
            P.op("act", lambda e: e.activation(out=rden[:, rs_, :], in_=rden[:, rs_, :], func=AF.Exp, scale=-1.0),
                 reads=[rden[:, rs_, :]], writes=[rden[:, rs_, :]])

        def attn_norm(t, i):
            hk, b = i // 4, i % 4
            ob, rs_ = 6, i % 2
            dst = mixA[:, hk * 4:(hk + 1) * 4, b * 128:(b + 1) * 128]
            P.op("dve", lambda e: e.tensor_tensor(
                out=dst, in0=ps[:, ob, :].rearrange("p (a b) -> p a b", b=128),
                in1=rden[:, rs_, :].rearrange("p (a b) -> p a b", b=128), op=ALU.mult),
                reads=[ps[:, ob, :], rden[:, rs_, :]], writes=[dst])

        def phase_BCD(t, base):
            seq = [("q", 4), ("q", 5), ("q", 6), ("q", 7)] + [("x", j) for j in range(16)]

            def next_piece():
                kind, j = seq.pop(0)
                if kind == "q":
                    q_piece(t, base, j, j % 4)
                else:
                    inproj_piece(t, base, j)
            pend["on"] = True
            for i in range(8):
                attn_scores(t, i, 0)
                if i >= 2:
                    attn_norm(t, i - 2)
                attn_scores(t, i, 1)
                next_piece()
                attn_scores(t, i, 2)
                next_piece()
                if i < 4:
                    next_piece()
                if i >= 1:
                    attn_pv(t, i - 1)
                act_flush()
                for b in range(NBLK):
                    sidx = i - 3 - b
                    if 0 <= sidx <= 2:
                        gv_ln_stage(b, sidx)
                act_flush()
            pend["on"] = False
            assert not seq
            attn_norm(t, 6)
            attn_pv(t, 7)
            attn_norm(t, 7)
            gv_ln_stage(3, 2)
            st["bank"] = 0
            for b in range(NBLK):
                p0 = bank2()

                def f(e, b=b, p0=p0):
                    for g in range(8):
                        ins = e.matmul(ps[:, p0 + g // 4, (g % 4) * 128:(g % 4 + 1) * 128],
                                       lhsT=gv[:, b, g * 128:(g + 1) * 128], rhs=wsT[:, g, :],
                                       start=True, stop=True)
                    return ins
                P.op("pe", f, reads=[gv[:, b, :], wsT[:]], writes=[ps[:, p0, :], ps[:, p0 + 1, :]])
                mx = ps[:, p0:p0 + 2, :].rearrange("p a b -> p (a b)")
                dst = xTm[:, 8:16, b * 128:(b + 1) * 128]
                usl = QU[:, 8:16, b * 128:(b + 1) * 128]
                z = zg(b)
                P.op("dve", lambda e, z=z, mx=mx: e.tensor_tensor(out=z, in0=mx, in1=bsb[:], op=ALU.add),
                     reads=[mx, bsb[:]], writes=[z])
                P.op("dve", lambda e, z=z, dst=dst, usl=usl: e.tensor_tensor(
                    out=dst, in0=z.rearrange("p (a b) -> p a b", b=128), in1=usl, op=ALU.mult),
                    reads=[z, usl], writes=[dst])

        def ln_apply(buf, mean, rstd):
            P.op("dve", lambda e: e.tensor_scalar(out=buf, in0=buf, scalar1=mean, scalar2=rstd,
                                                  op0=ALU.subtract, op1=ALU.mult),
                 reads=[buf, mean, rstd], writes=[buf])
            P.op("dve", lambda e: e.tensor_tensor(out=buf, in0=buf, in1=lnp[:, 0, :], op=ALU.mult),
                 reads=[buf, lnp[:, 0, :]], writes=[buf])
            P.op("dve", lambda e: e.tensor_tensor(out=buf, in0=buf, in1=lnp[:, 1, :], op=ALU.add),
                 reads=[buf, lnp[:, 1, :]], writes=[buf])

        def load_lnp(src):
            for i in range(2):
                P.op("sp", lambda e, i=i: e.dma_start(out=lnp[:, i, :], in_=src[:, i * D:(i + 1) * D]),
                     reads=[src[:, i * D:(i + 1) * D]], writes=[lnp[:, i, :]], dma_key=("lnp", i))

        def phase_E(t, base):
            load_lnp(ln1_d)
            for b in range(NBLK):
                src = xin[t, 1 + b]
                P.op("sp", lambda e, b=b, src=src: e.dma_start(out=x1[:, b, :], in_=src),
                     reads=[src], writes=[x1[:, b, :]], dma_key=("xres", b))

            def mm_pass(bp, nh):
                b0 = bank4()
                for kp in range(8):
                    pidx = base + (bp * 2 + nh) * 8 + kp
                    pc = piece_ap(pidx).rearrange("p (a b) -> p a b", b=1024)

                    def f(e, pc=pc, kp=kp):
                        for kk in range(2):
                            k = 2 * kp + kk
                            for bl in range(2):
                                for n in range(2):
                                    ins = e.matmul(ps[:, b0 + bl * 2 + n, :], lhsT=mixT_kb(k, 2 * bp + bl),
                                                   rhs=pc[:, kk, n * 512:(n + 1) * 512],
                                                   start=(k == 0), stop=(k == KC - 1))
                        return ins
                    bsl = slice(2 * bp * 128, (2 * bp + 2) * 128)
                    msrc = mixA[:, 2 * kp:2 * kp + 2, bsl] if kp < 4 else xTm[:, 2 * kp:2 * kp + 2, bsl]
                    P.op("pe", f, reads=[pc, msrc], writes=[ps[:, b0 + i, :] for i in range(4)])
                    done(pidx)
                for bl in range(2):
                    for n in range(2):
                        c0 = nh * 1024 + n * 512
                        sl = x1[:, 2 * bp + bl, c0:c0 + 512]
                        bk = b0 + bl * 2 + n
                        P.op("dve", lambda e, sl=sl, bk=bk: e.scalar_tensor_tensor(
                            out=sl, in0=sl, scalar=ALPHA, in1=ps[:, bk, :], op0=ALU.mult, op1=ALU.add),
                            reads=[sl, ps[:, bk, :]], writes=[sl])

            def ln1_pair(bp):
                blocks = [2 * bp, 2 * bp + 1]
                mr = ln_group([x1[:, b, :] for b in blocks], D)
                for i, b in enumerate(blocks):
                    ln_apply(x1[:, b, :], mr[i][0], mr[i][1])

            def xpose_pair(bp):
                for b in (2 * bp, 2 * bp + 1):
                    for q in range(4):
                        bk = bank()

                        def f(e, b=b, q=q, bk=bk):
                            for j in range(4):
                                k = q * 4 + j
                                ins = e.transpose(out=ps[:, bk, j * 128:(j + 1) * 128],
                                                  in_=x1[:, b, k * 128:(k + 1) * 128], identity=ident[:])
                            return ins
                        P.op("pe", f, reads=[x1[:, b, q * 512:(q + 1) * 512], ident[:]], writes=[ps[:, bk, :]])
                        dst = QU[:, q * 4:q * 4 + 4, b * 128:(b + 1) * 128]
                        src = ps[:, bk, :].rearrange("p (a b) -> p a b", b=128)
                        P.op("act", lambda e, dst=dst, src=src: e.activation(out=dst, in_=src, func=AF.Copy),
                             reads=[src], writes=[dst])

            mm_pass(0, 0)
            mm_pass(0, 1)
            ln1_pair(0)
            mm_pass(1, 0)
            mm_pass(1, 1)
            xpose_pair(0)
            ln1_pair(1)
            xpose_pair(1)

        def ln2_store(t, blocks):
            mr = ln_group([x1[:, b, :] for b in blocks], D)
            for i, b in enumerate(blocks):
                buf = x1[:, b, :]
                ln_apply(buf, mr[i][0], mr[i][1])
                dst = yc[t * TT + b * 128:t * TT + (b + 1) * 128, :]
                P.op("pool", lambda e, dst=dst, buf=buf: e.dma_start(out=dst, in_=buf),
                     reads=[buf], writes=[dst], dma_key=("out", b))

        def phase_F(t, base, a0gen):
            last_tile = (t == ntiles - 1)
            for kind, g in ffn_order:
                pb = base + ffn_base[(kind, g)] - ffn_base[("gu", 0)]
                if kind == "gu":
                    for which in range(2):
                        b0 = bank4()
                        for kq in range(4):
                            pc = piece_ap(pb + which * 4 + kq).rearrange("p (a b) -> p a b", b=512)

                            def f(e, pc=pc, kq=kq, b0=b0):
                                for kk in range(4):
                                    k = 4 * kq + kk
                                    for m in range(4):
                                        ins = e.matmul(ps[:, b0 + m, :], lhsT=pc[:, kk, m * 128:(m + 1) * 128],
                                                       rhs=QU[:, k, :], start=(k == 0), stop=(k == KC - 1))
                                return ins
                            P.op("pe", f, reads=[pc, QU[:, 4 * kq:4 * kq + 4, :]],
                                 writes=[ps[:, b0 + m, :] for m in range(4)])
                            done(pb + which * 4 + kq)
                        for m in range(4):
                            sgm = sg[:, m, :] if m < 2 else ein[:, m - 2, :]
                            if which == 0:
                                P.op("act", lambda e, sgm=sgm, b0=b0, m=m: e.activation(
                                    out=sgm, in_=ps[:, b0 + m, :], func=AF.Silu),
                                    reads=[ps[:, b0 + m, :]], writes=[sgm])
                            else:
                                dst = h1T[:, g % 2, m, :]
                                P.op("dve", lambda e, sgm=sgm, b0=b0, m=m, dst=dst: e.tensor_tensor(
                                    out=dst, in0=sgm, in1=ps[:, b0 + m, :], op=ALU.mult),
                                    reads=[sgm, ps[:, b0 + m, :]], writes=[dst])
                    if a0gen is not None:
                        next(a0gen, None)
                else:
                    pcs = [piece_ap(pb + kk) for kk in range(4)]
                    for b in range(NBLK):
                        b0 = bank4()

                        def f(e, b=b, b0=b0, pcs=pcs, g=g):
                            for kk in range(4):
                                for n in range(4):
                                    ins = e.matmul(ps[:, b0 + n, :], lhsT=h1T[:, g % 2, kk, b * 128:(b + 1) * 128],
                                                   rhs=pcs[kk][:, n * 512:(n + 1) * 512],
                                                   start=(kk == 0), stop=(kk == 3))
                            return ins
                        P.op("pe", f, reads=pcs + [h1T[:, g % 2, :, b * 128:(b + 1) * 128]],
                             writes=[ps[:, b0 + n, :] for n in range(4)])
                        for n in range(4):
                            sl = x1[:, b, n * 512:(n + 1) * 512]
                            if g == 0:
                                P.op("dve", lambda e, sl=sl, b0=b0, n=n: e.scalar_tensor_tensor(
                                    out=sl, in0=sl, scalar=ALPHA, in1=ps[:, b0 + n, :], op0=ALU.mult, op1=ALU.add),
                                    reads=[sl, ps[:, b0 + n, :]], writes=[sl])
                            else:
                                P.op("dve", lambda e, sl=sl, b0=b0, n=n: e.tensor_tensor(
                                    out=sl, in0=sl, in1=ps[:, b0 + n, :], op=ALU.add),
                                    reads=[sl, ps[:, b0 + n, :]], writes=[sl])
                        if last_tile and g == NG - 1:
                            ln2_store(t, [b])
                    done(pb + 3)
                    if g == NG - 3:
                        load_lnp(ln2_d)
            if a0gen is not None:
                for _ in a0gen:
                    pass
            if not last_tile:
                ln2_store(t, list(range(NBLK)))

        def tap(name, ap_sb, shape, dt):
            if name not in dbg:
                return
            dt_ = nc.dram_tensor("dbg_" + name, list(shape), dt, kind="ExternalOutput").ap()
            taps[name] = dt_
            P.op("sp", lambda e: e.dma_start(out=dt_, in_=ap_sb), reads=[ap_sb], writes=[dt_],
                 dma_key=("dbg", name))
            P.final_keys.append(("dbg", name))

        emit_consts()
        for _ in a0_steps(0):
            pass
        for t in range(ntiles):
            base = t * NPIECE
            phase_A(t, base)
            if t == 0:
                tap("QU_A", QU[:], [128, KC, TT], BF16)
                tap("kT", kT[:], [128, 2, 768], BF16)
                tap("vv", vv[:], [128, 6, 256], BF16)
                tap("gv", gv[:], [128, NBLK, 1024], BF16)
            if stop_after == "A":
                break
            phase_BCD(t, base)
            if t == 0:
                tap("mixT", xTm[:], [128, KC, TT], BF16)
                tap("mixA", mixA, [128, 8, TT], BF16)
                tap("gvD", gv[:], [128, NBLK, 1024], BF16)
                tap("QUD", QU[:], [128, KC, TT], BF16)
            if stop_after == "D":
                break
            phase_E(t, base + 28)
            if t == 0:
                tap("x1", x1[:], [128, NBLK, D], F32)
                tap("x1T", QU[:], [128, KC, TT], BF16)
            if stop_after == "E":
                break
            gen = a0_steps(t + 1) if t + 1 < ntiles else None
            phase_F(t, base + 60, gen)
        if stop_after == "F":
            for b in range(NBLK):
                P.final_keys.append(("out", b))

        keys = [("eng", e) for e in Prog.ENGS] + [("dma", k) for k in P.dma_keys()]
        sems = {}
        for i, k in enumerate(keys):
            sems[k] = es.enter_context(nc.semaphore("s%d" % i))
        run = P.emit(nc, sems)
        block = es.enter_context(nc.Block())

        @block.tensor
        def _(e):
            run("pe", e)

        @block.scalar
        def _(e):
            run("act", e)

        @block.vector
        def _(e):
            run("dve", e)

        @block.gpsimd
        def _(e):
            run("pool", e)

        @block.sync
        def _(e):
            run("sp", e)
    nc._prog_stats = {"nops": len(P.ops), "counts": P.counts, "nsems": len(keys)}
    return nc


def _alibi_consts():
    kk = np.arange(128)[:, None]
    qq = np.arange(128)[None, :]
    dm = np.zeros((128, 3, 128), np.float32)
    for j in range(3):
        dist = np.abs(kk + (j - 1) * 128 - qq)
        dm[:, j, :] = np.where(dist <= 128, dist, 1.0e6)
    h = np.arange(1, 9, dtype=np.float32)
    slopes = (2.0 ** (-8.0 * h / 8)).astype(np.float32)
    nsl = np.broadcast_to(-slopes[None, :], (128, 8)).astype(np.float32)
    return dm.reshape(128, 384), np.ascontiguousarray(nsl)


def _rep(v, n=128):
    v = np.asarray(v, np.float32).reshape(1, -1)
    return np.ascontiguousarray(np.broadcast_to(v, (n, v.shape[1])))


def make_in_maps(x_prompt, x_sample, w_in, ln_sgu_g, ln_sgu_b, w_s, b_s, attn_sink, w_o,
                 ln1_g, ln1_b, w_gate, w_up, w_down, ln2_g, ln2_b):
    f = lambda a: np.ascontiguousarray(np.asarray(a, np.float32))
    xp, xsm = f(x_prompt), f(x_sample)
    dm, nsl = _alibi_consts()
    shared = {
        "w_in": f(w_in[0]), "w_o": f(w_o[0]), "w_gate": f(w_gate[0]), "w_up": f(w_up[0]),
        "w_down": f(w_down[0]),
        "distm": dm, "nslope": nsl,
        "sink": _rep(attn_sink[0]),
        "bsb": _rep(np.asarray(b_s[0]).reshape(-1)),
        "lnsg": _rep(ln_sgu_g[0]), "lnsb": _rep(ln_sgu_b[0]),
        "ln1": np.concatenate([_rep(ln1_g[0]), _rep(ln1_b[0])], axis=1),
        "ln2": np.concatenate([_rep(ln2_g[0]), _rep(ln2_b[0])], axis=1),
        "wst": f(np.transpose(np.asarray(w_s[0], np.float32), (2, 0, 1)).reshape(128, 1024)),
        "ident": np.eye(128, dtype=np.float32),
    }
    in_maps = []
    for c in range(8):
        segs = [(xp[c], 0, 2048), (xsm[c // 2], (c % 2) * 1024, (c % 2) * 1024 + 1024)]
        xin = np.zeros((NTILE, 6, 128, D), np.float32)
        hbv = np.zeros((12,), np.float32)
        t = 0
        for seq, lo, hi in segs:
            for s0 in range(lo, hi, TT):
                xin[t, 1:5] = seq[s0:s0 + TT].reshape(4, 128, D)
                if s0 - 128 >= 0:
                    xin[t, 0] = seq[s0 - 128:s0]
                else:
                    hbv[2 * t] = NEGB
                if s0 + TT + 128 <= seq.shape[0]:
                    xin[t, 5] = seq[s0 + TT:s0 + TT + 128]
                else:
                    hbv[2 * t + 1] = NEGB
                t += 1
        m = dict(shared)
        m["xin"] = xin
        m["hb"] = _rep(hbv)
        in_maps.append(m)
    return in_maps


def gather_outputs(results):
    yp = np.zeros((8, 2048, D), np.float32)
    ysm = np.zeros((4, 2048, D), np.float32)
    for c in range(8):
        y = np.asarray(results[c]["yc"], np.float32)
        yp[c] = y[0:2048]
        ysm[c // 2, (c % 2) * 1024:(c % 2) * 1024 + 1024] = y[2048:3072]
    return yp, ysm


_NC_CACHE = {}


def kernel(x_prompt, x_sample, w_in, ln_sgu_g, ln_sgu_b, w_s, b_s, attn_sink, w_o,
           ln1_g, ln1_b, w_gate, w_up, w_down, ln2_g, ln2_b):
    in_maps = make_in_maps(x_prompt, x_sample, w_in, ln_sgu_g, ln_sgu_b, w_s, b_s, attn_sink, w_o,
                           ln1_g, ln1_b, w_gate, w_up, w_down, ln2_g, ln2_b)
    nc = build()
    res = run_bass_kernel_spmd(nc, in_maps, core_ids=list(range(8)))
    return gather_outputs(res.results)
```
